# Optimizing a Trainium2 kernel written in Bass

```python
import math
import jax, jax.numpy as jnp
from jax import lax
import numpy as np

D_MODEL = 1024
BATCH = 16
SEQ = 2048
DEPTH = 1

D_MIX = D_MODEL
ATT_WIDTH = D_MIX // 2
SSD_WIDTH = D_MIX - ATT_WIDTH
HEAD_DIM = 64
N_Q_HEADS = ATT_WIDTH // HEAD_DIM
N_KV_HEADS = 2
Q_PER_KV = N_Q_HEADS // N_KV_HEADS
ROPE_AXIS_DIM = HEAD_DIM // 2
ROPE_THETA = 10000.0
Q_BLOCK = 128
GRID_W = 64
SSD_HEAD_DIM = 64
SSD_HEADS = SSD_WIDTH // SSD_HEAD_DIM
SSD_GROUPS = 2
HEADS_PER_GROUP = SSD_HEADS // SSD_GROUPS
D_STATE = 128
CONV_WIDTH = 5
CONV_CH = SSD_WIDTH + 2 * SSD_GROUPS * D_STATE
CHUNK = 128
N_DIRS = 2
D_FF = 2816
PLE_DIM = 256
N_NORMS = 4
ALPHA = (2.0 * DEPTH) ** 0.25
BETA = (8.0 * DEPTH) ** -0.25
SPLIT_SIZES = [N_Q_HEADS * HEAD_DIM, N_KV_HEADS * HEAD_DIM, N_KV_HEADS * HEAD_DIM,
               SSD_WIDTH, SSD_WIDTH, SSD_GROUPS * D_STATE, SSD_GROUPS * D_STATE,
               N_DIRS * SSD_HEADS]
IN_PROJ_WIDTH = sum(SPLIT_SIZES)
SPLIT_POINTS = [int(v) for v in np.cumsum(SPLIT_SIZES)[:-1]]
LN_EPS = 1e-5
RMS_EPS = 1e-6

kernel_name = "hymba_attn_ssd_macaron_deepnorm_block"


def _layer_norm(x, g, b):
    xf = x.astype(jnp.float32)
    mu = jnp.mean(xf, axis=-1, keepdims=True)
    var = jnp.mean(jnp.square(xf - mu), axis=-1, keepdims=True)
    y = (xf - mu) * lax.rsqrt(var + LN_EPS) * g.astype(jnp.float32) + b.astype(jnp.float32)
    return y.astype(x.dtype)


def _rms(xf, w):
    return xf * lax.rsqrt(jnp.mean(jnp.square(xf), axis=-1, keepdims=True) + RMS_EPS) * w.astype(jnp.float32)


def _swiglu(x, w_in, w_out):
    gate, up = jnp.split(x @ w_in, 2, axis=-1)
    return (jax.nn.silu(gate) * up) @ w_out


def _axial_rope_tables(S):
    rows = S // GRID_W
    row = jnp.repeat(jnp.arange(rows, dtype=jnp.float32), GRID_W)
    col = jnp.tile(jnp.arange(GRID_W, dtype=jnp.float32), rows)
    inv = ROPE_THETA ** (-jnp.arange(0, ROPE_AXIS_DIM, 2, dtype=jnp.float32) / ROPE_AXIS_DIM)
    ang = jnp.concatenate([row[:, None] * inv, col[:, None] * inv], axis=-1)
    return jnp.cos(ang), jnp.sin(ang)


def _apply_rope(xf, cos, sin):
    xr = xf.reshape(*xf.shape[:-1], HEAD_DIM // 2, 2)
    x0, x1 = xr[..., 0], xr[..., 1]
    c = cos[None, :, None, :]
    s = sin[None, :, None, :]
    return jnp.stack([x0 * c - x1 * s, x0 * s + x1 * c], axis=-1).reshape(xf.shape)


def _block_attention(q, k, v):
    B, S = q.shape[0], q.shape[1]
    nblk = S // Q_BLOCK
    scale = HEAD_DIM ** -0.5
    qb = q.reshape(B, nblk, Q_BLOCK, N_KV_HEADS, Q_PER_KV, HEAD_DIM).transpose(1, 0, 2, 3, 4, 5)

    def one_block(qi):
        s = jnp.einsum('bqhgd,bkhd->bhgqk', qi, k, preferred_element_type=jnp.float32) * scale
        pr = jax.nn.softmax(s, axis=-1)
        return jnp.einsum('bhgqk,bkhd->bqhgd', pr.astype(v.dtype), v)

    out = lax.map(one_block, qb)
    return out.transpose(1, 0, 2, 3, 4, 5).reshape(B, S, N_Q_HEADS * HEAD_DIM)


def _ssd_scan(xh, dt, a, bm, cm):
    B, S = xh.shape[0], xh.shape[1]
    nc = S // CHUNK
    G, R, P, N = SSD_GROUPS, HEADS_PER_GROUP, SSD_HEAD_DIM, D_STATE
    xd = (xh * dt[..., None]).reshape(B, nc, CHUNK, G, R, P)
    adt = (dt * a).reshape(B, nc, CHUNK, G, R).transpose(0, 3, 4, 1, 2)
    a_cum = jnp.cumsum(adt, axis=-1)
    b_c = bm.reshape(B, nc, CHUNK, G, N)
    c_c = cm.reshape(B, nc, CHUNK, G, N)
    tril = jnp.tril(jnp.ones((CHUNK, CHUNK), dtype=bool))
    seg = a_cum[..., :, None] - a_cum[..., None, :]
    Lmat = jnp.exp(jnp.where(tril, seg, -jnp.inf))
    cb = jnp.einsum('bclgn,bcsgn->bcgls', c_c, b_c)
    y_diag = jnp.einsum('bcgls,bgrcls,bcsgrp->bclgrp', cb, Lmat, xd)
    decay_states = jnp.exp(a_cum[..., -1:] - a_cum)
    states = jnp.einsum('bcsgn,bgrcs,bcsgrp->bcgrpn', b_c, decay_states, xd)
    chunk_decay = jnp.exp(a_cum[..., -1])

    def step(h, inp):
        st, d = inp
        return h * d[..., None, None] + st, h

    h0 = jnp.zeros((B, G, R, P, N), jnp.float32)
    _, prev = lax.scan(step, h0, (jnp.moveaxis(states, 1, 0), jnp.moveaxis(chunk_decay, 3, 0)))
    prev = jnp.moveaxis(prev, 0, 1)
    y_off = jnp.einsum('bclgn,bcgrpn,bgrcl->bclgrp', c_c, prev, jnp.exp(a_cum))
    return (y_diag + y_off).reshape(B, S, G, R, P)


def setup_inputs(seed: int = 0) -> dict:
    key = jax.random.key(seed)
    ks = jax.random.split(key, 21)
    L = DEPTH
    nrm = jax.random.normal
    x = nrm(ks[0], (BATCH, SEQ, D_MODEL), jnp.float32)
    p = nrm(ks[1], (DEPTH, BATCH, SEQ, PLE_DIM), jnp.float32)
    ln_g = 1.0 + 0.02 * nrm(ks[2], (L, N_NORMS, D_MODEL), jnp.float32)
    ln_b = 0.02 * nrm(ks[3], (L, N_NORMS, D_MODEL), jnp.float32)
    ffn1_w_in = nrm(ks[4], (L, D_MODEL, 2 * D_FF), jnp.float32) * D_MODEL ** -0.5
    ffn1_w_out = nrm(ks[5], (L, D_FF, D_MODEL), jnp.float32) * (D_FF ** -0.5 * BETA)
    w_in = nrm(ks[6], (L, D_MODEL, IN_PROJ_WIDTH), jnp.float32) * D_MODEL ** -0.5
    q_norm = 1.0 + 0.02 * nrm(ks[7], (L, HEAD_DIM), jnp.float32)
    k_norm = 1.0 + 0.02 * nrm(ks[8], (L, HEAD_DIM), jnp.float32)
    conv_w = nrm(ks[9], (L, CONV_WIDTH, CONV_CH), jnp.float32) * CONV_WIDTH ** -0.5
    conv_b = 0.02 * nrm(ks[10], (L, CONV_CH), jnp.float32)
    dt0 = jnp.exp(jax.random.uniform(ks[11], (L, N_DIRS, SSD_HEADS), jnp.float32,
                                     minval=math.log(1e-3), maxval=math.log(1e-1)))
    dt_bias = dt0 + jnp.log(-jnp.expm1(-dt0))
    a_log = jnp.log(jax.random.uniform(ks[12], (L, N_DIRS, SSD_HEADS), jnp.float32, minval=1.0, maxval=16.0))
    d_skip = 1.0 + 0.1 * nrm(ks[13], (L, SSD_HEADS), jnp.float32)
    ssd_norm = 1.0 + 0.02 * nrm(ks[14], (L, SSD_WIDTH), jnp.float32)
    w_out = nrm(ks[15], (L, D_MIX, D_MODEL), jnp.float32) * (D_MIX ** -0.5 * BETA)
    ffn2_w_in = nrm(ks[16], (L, D_MODEL, 2 * D_FF), jnp.float32) * D_MODEL ** -0.5
    ffn2_w_out = nrm(ks[17], (L, D_FF, D_MODEL), jnp.float32) * (D_FF ** -0.5 * BETA)
    ple_w = nrm(ks[18], (L, PLE_DIM, D_MODEL), jnp.float32) * (PLE_DIM ** -0.5 * BETA)
    ple_gate_w = nrm(ks[19], (L, D_MODEL, D_MODEL), jnp.float32) * D_MODEL ** -0.5
    ple_gate_b = 0.02 * nrm(ks[20], (L, D_MODEL), jnp.float32)
    return {"x": x, "p": p, "ln_g": ln_g, "ln_b": ln_b,
            "ffn1_w_in": ffn1_w_in, "ffn1_w_out": ffn1_w_out,
            "w_in": w_in, "q_norm": q_norm, "k_norm": k_norm,
            "conv_w": conv_w, "conv_b": conv_b, "dt_bias": dt_bias, "a_log": a_log,
            "d_skip": d_skip, "ssd_norm": ssd_norm, "w_out": w_out,
            "ffn2_w_in": ffn2_w_in, "ffn2_w_out": ffn2_w_out,
            "ple_w": ple_w, "ple_gate_w": ple_gate_w, "ple_gate_b": ple_gate_b}


def reference(x, p, ln_g, ln_b, ffn1_w_in, ffn1_w_out, w_in, q_norm, k_norm,
              conv_w, conv_b, dt_bias, a_log, d_skip, ssd_norm, w_out,
              ffn2_w_in, ffn2_w_out, ple_w, ple_gate_w, ple_gate_b):
    B, S, _ = x.shape
    cos, sin = _axial_rope_tables(S)
    G, R, P = SSD_GROUPS, HEADS_PER_GROUP, SSD_HEAD_DIM
    for i in range(DEPTH):
        x = _layer_norm(ALPHA * x + 0.5 * _swiglu(x, ffn1_w_in[i], ffn1_w_out[i]), ln_g[i, 0], ln_b[i, 0])

        u = x @ w_in[i]
        q, k, v, z, xs, bm, cm, dt_raw = jnp.split(u, SPLIT_POINTS, axis=-1)

        qf = _rms(q.reshape(B, S, N_Q_HEADS, HEAD_DIM).astype(jnp.float32), q_norm[i])
        kf = _rms(k.reshape(B, S, N_KV_HEADS, HEAD_DIM).astype(jnp.float32), k_norm[i])
        qh = _apply_rope(qf, cos, sin).astype(x.dtype)
        kh = _apply_rope(kf, cos, sin).astype(x.dtype)
        vh = v.reshape(B, S, N_KV_HEADS, HEAD_DIM)
        att_out = _block_attention(qh, kh, vh)

        xbc = jnp.concatenate([xs, bm, cm], axis=-1)
        cw = conv_w[i].astype(xbc.dtype).reshape(CONV_WIDTH, 1, CONV_CH)
        xbc = lax.conv_general_dilated(xbc, cw, window_strides=(1,),
                                       padding=[(CONV_WIDTH // 2, CONV_WIDTH // 2)],
                                       dimension_numbers=('NWC', 'WIO', 'NWC'),
                                       feature_group_count=CONV_CH)
        xbc = jax.nn.silu(xbc.astype(jnp.float32) + conv_b[i].astype(jnp.float32))
        xs_c, bm_c, cm_c = jnp.split(xbc, [SSD_WIDTH, SSD_WIDTH + G * D_STATE], axis=-1)
        xh = xs_c.reshape(B, S, G, R, P)
        bm_c = bm_c.reshape(B, S, G, D_STATE)
        cm_c = cm_c.reshape(B, S, G, D_STATE)
        dt_all = jax.nn.softplus(dt_raw.astype(jnp.float32).reshape(B, S, N_DIRS, G, R)
                                 + dt_bias[i].astype(jnp.float32).reshape(N_DIRS, G, R))
        a_all = -jnp.exp(a_log[i].astype(jnp.float32)).reshape(N_DIRS, G, R)
        y_f = _ssd_scan(xh, dt_all[:, :, 0], a_all[0], bm_c, cm_c)
        y_b = jnp.flip(_ssd_scan(jnp.flip(xh, 1), jnp.flip(dt_all[:, :, 1], 1), a_all[1],
                                 jnp.flip(bm_c, 1), jnp.flip(cm_c, 1)), 1)
        y = y_f + y_b + d_skip[i].astype(jnp.float32).reshape(G, R)[:, :, None] * xh
        y = y.reshape(B, S, SSD_WIDTH) * jax.nn.silu(z.astype(jnp.float32))
        y = _rms(y.reshape(B, S, G, SSD_WIDTH // G), jnp.ones((), jnp.float32)).reshape(B, S, SSD_WIDTH)
        ssd_out = (y * ssd_norm[i].astype(jnp.float32)).astype(x.dtype)

        mix = jnp.concatenate([att_out, ssd_out], axis=-1) @ w_out[i]
        x = _layer_norm(ALPHA * x + mix, ln_g[i, 1], ln_b[i, 1])

        x = _layer_norm(ALPHA * x + 0.5 * _swiglu(x, ffn2_w_in[i], ffn2_w_out[i]), ln_g[i, 2], ln_b[i, 2])

        e = p[i] @ ple_w[i]
        gate = jax.nn.sigmoid(x @ ple_gate_w[i] + ple_gate_b[i])
        x = _layer_norm(ALPHA * x + gate * e, ln_g[i, 3], ln_b[i, 3])
    return x
```

```python
import numpy as np
import concourse.bass as bass
import concourse.mybir as mybir
from concourse.bass_utils import run_bass_kernel_spmd
from contextlib import ExitStack

F32 = mybir.dt.float32
BF16 = mybir.dt.bfloat16
AF = mybir.ActivationFunctionType
ALU = mybir.AluOpType
AX = mybir.AxisListType

ENGS = ("pe", "act", "dve", "pool", "sp")


class Res:
    __slots__ = ("name", "writer", "readers", "excl")

    def __init__(self, name, excl=False):
        self.name = name
        self.writer = None
        self.readers = []
        self.excl = excl


class Op:
    __slots__ = ("idx", "eng", "fn", "deps", "signal", "sem", "count", "is_dma", "known")

    def __init__(self, idx, eng, fn, is_dma, sem):
        self.idx = idx
        self.eng = eng
        self.fn = fn
        self.deps = []
        self.signal = False
        self.sem = sem
        self.count = None
        self.is_dma = is_dma
        self.known = None


class Sched:
    def __init__(self, nc, stack):
        self.nc = nc
        self.stack = stack
        self.ops = []
        self.eng_sem = {e: stack.enter_context(nc.semaphore("sem_" + e)) for e in ENGS}
        self.dma_sems = {}
        self.dma_counts = {}
        self.nres = 0
        self.last_real = {}
        self.dmas_since_barrier = []

    def res(self, name=None, excl=False):
        self.nres += 1
        return Res(name or f"r{self.nres}", excl)

    def _dma_sem(self, key):
        if key not in self.dma_sems:
            self.dma_sems[key] = self.stack.enter_context(self.nc.semaphore("dsem_" + str(key)))
            self.dma_counts[key] = 0
        return self.dma_sems[key]

    def op(self, eng, meth, kwargs, reads=(), writes=(), dma_key=None):
        fn = (meth, kwargs)
        is_dma = dma_key is not None
        sem = self._dma_sem(dma_key) if is_dma else self.eng_sem[eng]
        o = Op(len(self.ops), eng, fn, is_dma, sem)
        if is_dma:
            self.dma_counts[dma_key] += 16
            o.count = self.dma_counts[dma_key]
            o.signal = True
            self.dmas_since_barrier.append(o)
        deps = {}
        wr = list(writes) + [r for r in reads if r.excl]
        for r in reads:
            if r.excl:
                continue
            if r.writer is not None:
                deps[r.writer.idx] = r.writer
        for w in wr:
            if w.writer is not None:
                deps[w.writer.idx] = w.writer
            for rd in w.readers:
                deps[rd.idx] = rd
        for r in reads:
            if not r.excl:
                if not is_dma:
                    r.readers = [x for x in r.readers if x.is_dma or x.eng != eng]
                r.readers.append(o)
        for w in wr:
            w.writer = o
            w.readers = []
        for p in deps.values():
            if p is o:
                continue
            if p.eng == "pe" and eng == "pe" and not p.is_dma and not is_dma:
                continue
            o.deps.append(p)
            p.signal = True
        self.ops.append(o)
        self.last_real[eng] = o
        return o

    def barrier(self):
        targets = [p for p in self.last_real.values()] + list(self.dmas_since_barrier)
        seen = set()
        tl = []
        for p in targets:
            if p.idx not in seen:
                seen.add(p.idx)
                tl.append(p)
        for e in ENGS:
            o = Op(len(self.ops), e, None, False, self.eng_sem[e])
            for p in tl:
                o.deps.append(p)
                p.signal = True
            self.ops.append(o)
        self.dmas_since_barrier = []

    def emit(self):
        nc = self.nc
        cnt = {e: 0 for e in ENGS}
        for o in self.ops:
            if not o.is_dma and o.signal and o.fn is not None:
                cnt[o.eng] += 1
                o.count = cnt[o.eng]
        known = {e: {} for e in ENGS}
        waits = {}
        for o in self.ops:
            k = known[o.eng]
            w = []
            for p in sorted(o.deps, key=lambda p: -p.idx):
                if k.get(p.sem, 0) >= p.count:
                    continue
                w.append((p.sem, p.count))
                k[p.sem] = p.count
                if p.known:
                    for s, c in p.known.items():
                        if k.get(s, 0) < c:
                            k[s] = c
            waits[o.idx] = w
            o.known = dict(k)
        per_eng = {e: [o for o in self.ops if o.eng == e] for e in ENGS}
        self.stats = {e: len(per_eng[e]) for e in ENGS}
        self.stats["waits"] = sum(len(w) for w in waits.values())
        self.stats["sems"] = len(self.dma_sems) + len(ENGS)
        self.stats["cnt"] = dict(cnt)

        def run(engname, eobj):
            for o in per_eng[engname]:
                for (s, c) in waits[o.idx]:
                    eobj.wait_ge(s, c)
                if o.fn is None:
                    continue
                ins = getattr(eobj, o.fn[0])(**o.fn[1])
                if o.is_dma:
                    ins.then_inc(o.sem, 16)
                elif o.signal:
                    ins.then_inc(o.sem, 1)

        with nc.Block() as block:
            @block.tensor
            def _(e):
                run("pe", e)

            @block.scalar
            def _(e):
                run("act", e)

            @block.vector
            def _(e):
                run("dve", e)

            @block.gpsimd
            def _(e):
                run("pool", e)

            @block.sync
            def _(e):
                run("sp", e)
                for key, s in self.dma_sems.items():
                    e.wait_ge(s, self.dma_counts[key])
                for en in ENGS:
                    if en != "sp" and cnt[en] > 0:
                        e.wait_ge(self.eng_sem[en], cnt[en])


D = 1024
SEQ = 2048
NT = 16
DFF = 2816
NF = 22
FG = 2
ALPHA = 2.0 ** 0.25
LN_EPS = 1e-5
RMS_EPS = 1e-6
SSD_STOP = [99]
C_Q, C_K, C_V, C_Z, C_XS, C_B, C_C, C_DT = 0, 512, 640, 768, 1280, 1792, 2048, 2304


def build_nc(nseq=2, stage=99, dbg=None):
    nc = bass.Bass("TRN2", target_bir_lowering=False)

    def din(name, shape):
        return nc.dram_tensor(name, list(shape), F32, kind="ExternalInput").ap()

    x_d = din("x", [nseq, SEQ, D])
    p_d = din("p", [nseq, SEQ, 256])
    ln_g = din("ln_g", [4, D])
    ln_b = din("ln_b", [4, D])
    w1i = din("ffn1_w_in", [D, 2 * DFF])
    w1o = din("ffn1_w_out", [DFF, D])
    w_in = din("w_in", [D, 2320])
    q_norm = din("q_norm", [64])
    k_norm = din("k_norm", [64])
    conv_w = din("conv_w", [5, 1024])
    conv_b = din("conv_b", [1024])
    dt_bias = din("dt_bias", [16])
    a_log = din("a_log", [16])
    d_skip = din("d_skip", [8])
    ssd_norm = din("ssd_norm", [512])
    w_out = din("w_out", [D, D])
    w2i = din("ffn2_w_in", [D, 2 * DFF])
    w2o = din("ffn2_w_out", [DFF, D])
    ple_w = din("ple_w", [256, D])
    ple_gw = din("ple_gate_w", [D, D])
    ple_gb = din("ple_gate_b", [D])
    c_ident = din("c_ident", [128, 128])
    c_masks = din("c_masks", [4, 128, 128])
    c_cos = din("c_cos", [SEQ, 32])
    c_sin = din("c_sin", [SEQ, 32])
    out_d = nc.dram_tensor("out", [nseq, SEQ, D], F32, kind="ExternalOutput").ap()
    dbg_d = None
    if dbg is not None:
        dbg_d = nc.dram_tensor("dbg", [SEQ, D], F32, kind="ExternalOutput").ap()

    with ExitStack() as stack:
        S = Sched(nc, stack)

        uid = [0]

        def sbt(st, name, shape, dt):
            uid[0] += 1
            return st.enter_context(nc.sbuf_tensor(f"{name}_{uid[0]}", list(shape), dt))

        def MM(out, lhsT, rhs, start, stop, reads, writes):
            S.op("pe", "matmul", dict(out=out, lhsT=lhsT, rhs=rhs, start=start, stop=stop), reads, writes)

        def TR(out, in_, ident, reads, writes):
            S.op("pe", "transpose", dict(out=out, in_=in_, identity=ident), list(reads) + [r_const], writes)

        def ACT(out, in_, func, reads, writes, bias=None, scale=None, accum_out=None):
            kw = dict(out=out, in_=in_, func=func)
            if bias is not None:
                kw["bias"] = bias
            if scale is not None:
                kw["scale"] = scale
            if accum_out is not None:
                kw["accum_out"] = accum_out
            S.op("act", "activation", kw, reads, writes)

        def ACTcopy(out, in_, reads, writes):
            S.op("act", "copy", dict(out=out, in_=in_), reads, writes)

        def TT(eng, out, in0, in1, op, reads, writes):
            S.op(eng, "tensor_tensor", dict(out=out, in0=in0, in1=in1, op=op), reads, writes)

        def TS(eng, out, in0, s1, s2, op0, op1, reads, writes):
            kw = dict(out=out, in0=in0, scalar1=s1, scalar2=s2, op0=op0)
            if op1 is not None:
                kw["op1"] = op1
            S.op(eng, "tensor_scalar", kw, reads, writes)

        def STT(eng, out, in0, scalar, in1, op0, op1, reads, writes):
            S.op(eng, "scalar_tensor_tensor", dict(out=out, in0=in0, scalar=scalar, in1=in1, op0=op0, op1=op1), reads, writes)

        def CP(eng, out, in_, reads, writes):
            S.op(eng, "tensor_copy", dict(out=out, in_=in_), reads, writes)

        def MEMSET(eng, ap, val, writes):
            S.op(eng, "memset", dict(ap=ap, constant=val), (), writes)

        def DMA(eng, out, in_, key, reads, writes, slow=False):
            kw = dict(out=out, in_=in_)
            if slow:
                kw["allow_slow_non_contiguous"] = True
            S.op(eng, "dma_start", kw, reads, writes, dma_key=key)

        acc = sbt(stack, "acc", [128, NT, D], F32)
        r_acc = [S.res(f"acc{t}") for t in range(NT)]
        xT = sbt(stack, "xT", [128, 8, SEQ], BF16)
        r_xT = [S.res(f"xT{t}") for t in range(NT)]
        identf = sbt(stack, "identf", [128, 128], F32)
        identb = sbt(stack, "identb", [128, 128], BF16)
        masks = sbt(stack, "masks", [128, 4, 128], F32)
        onesf = sbt(stack, "onesf", [128, 128], F32)
        cwall = sbt(stack, "cwall", [128, 8, 5], F32)
        cball = sbt(stack, "cball", [128, 8], F32)
        dtb_bc = sbt(stack, "dtb_bc", [128, 16], F32)
        A_bc = sbt(stack, "A_bc", [128, 16], F32)
        dsk_bc = sbt(stack, "dsk_bc", [128, 8], F32)
        nw = sbt(stack, "nw", [128, 2, 64], F32)
        snw = sbt(stack, "snw", [128, 512], F32)
        r_const = S.res("const")
        banks = [stack.enter_context(nc.psum_tensor(f"bank{i}", [128, 512], F32)) for i in range(8)]
        r_bank = [S.res(f"bank{i}", excl=True) for i in range(8)]
        M_LE, M_GE, M_GT, M_LT = (masks[:, i, :] for i in range(4))

        DMA("sp", identf[:], c_ident, "c0", [], [r_const])
        DMA("sp", masks[:], c_masks.rearrange("m p j -> p m j"), "c1", [], [r_const])
        for jj in range(5):
            DMA("sp", cwall[:, :, jj], conv_w[jj].rearrange("(cc p) -> p cc", p=128), "c2", [], [r_const], slow=True)
        DMA("sp", cball[:], conv_b.rearrange("(cc p) -> p cc", p=128), "c3", [], [r_const], slow=True)
        DMA("sp", dtb_bc[:], dt_bias.partition_broadcast(128), "c4", [], [r_const])
        DMA("sp", A_bc[:], a_log.partition_broadcast(128), "c5", [], [r_const])
        DMA("sp", dsk_bc[:], d_skip.partition_broadcast(128), "c6", [], [r_const])
        DMA("sp", nw[:, 0, :], q_norm.partition_broadcast(128), "c7", [], [r_const])
        DMA("sp", nw[:, 1, :], k_norm.partition_broadcast(128), "c8", [], [r_const])
        DMA("sp", snw[:], ssd_norm.partition_broadcast(128), "c9", [], [r_const])
        CP("dve", identb[:], identf[:], [r_const], [r_const])
        MEMSET("dve", onesf[:], 1.0, [r_const])
        ACT(A_bc[:], A_bc[:], AF.Exp, [r_const], [r_const])
        TS("dve", A_bc[:], A_bc[:], -1.0, None, ALU.mult, None, [r_const], [r_const])

        class Rot:
            def __init__(self, st, name, shape, dt, n):
                self.tiles = [sbt(st, f"{name}{i}", shape, dt) for i in range(n)]
                self.res = [S.res(f"{name}{i}") for i in range(n)]
                self.keys = [f"{name}{i}" for i in range(n)]
                self.n = n
                self.i = -1

            def next(self):
                self.i = (self.i + 1) % self.n
                return self.tiles[self.i], self.res[self.i], self.keys[self.i]

        class BankRot:
            def __init__(self, ids):
                self.ids = ids
                self.i = -1

            def next(self):
                self.i = (self.i + 1) % len(self.ids)
                b = self.ids[self.i]
                return banks[b], r_bank[b]

        def bf16view(bk, a):
            return bk[:].bitcast(BF16).rearrange("p (a b) -> p a b", a=a)

        def prep(t, xbrot, bankrot):
            xb, r_xb, _ = xbrot.next()
            CP("dve", xb[:], acc[:, t, :], [r_acc[t]], [r_xb])
            S.op("act", "mul", dict(out=acc[:, t, :], in_=acc[:, t, :], mul=ALPHA), [r_acc[t]], [r_acc[t]])
            bk, r_bk = bankrot.next()
            bkb = bf16view(bk, 8)
            for kc in range(8):
                TR(bkb[:, kc, :], xb[:, kc * 128:(kc + 1) * 128], identb[:], [r_xb], [r_bk])
            ACTcopy(xT[:, :, t * 128:(t + 1) * 128], bkb, [r_bk], [r_xT[t]])

        def layernorm(t, lnrot, g_bc, b_bc, r_ln):
            st6, r_st, _ = lnrot.next()
            bst = st6[:, 0:12].rearrange("p (a b) -> p a b", a=2)
            mv = st6[:, 12:14]
            tmp = st6[:, 14:16]
            for hf in range(2):
                S.op("dve", "bn_stats", dict(out=bst[:, hf, :], in_=acc[:, t, hf * 512:(hf + 1) * 512]), [r_acc[t]], [r_st])
            S.op("dve", "bn_aggr", dict(out=mv, in_=bst), [r_st], [r_st])
            ACT(tmp[:, 0:1], mv[:, 1:2], AF.Ln, [r_st], [r_st], bias=LN_EPS)
            ACT(tmp[:, 0:1], tmp[:, 0:1], AF.Exp, [r_st], [r_st], scale=-0.5)
            TS("dve", tmp[:, 1:2], mv[:, 0:1], tmp[:, 0:1], -1.0, ALU.mult, ALU.mult, [r_st], [r_st])
            ACT(acc[:, t, :], acc[:, t, :], AF.Identity, [r_st, r_acc[t]], [r_acc[t]], bias=tmp[:, 1:2], scale=tmp[:, 0:1])
            TT("pool", acc[:, t, :], acc[:, t, :], g_bc[:], ALU.mult, [r_acc[t], r_ln], [r_acc[t]])
            TT("pool", acc[:, t, :], acc[:, t, :], b_bc[:], ALU.add, [r_acc[t], r_ln], [r_acc[t]])

        def load_ln(idx, g_bc, b_bc, r_ln):
            DMA("sp", g_bc[:], ln_g[idx].partition_broadcast(128), "lng", [], [r_ln])
            DMA("sp", b_bc[:], ln_b[idx].partition_broadcast(128), "lnb", [], [r_ln])

        def ffn_phase(wi_d, wo_d, ln_idx, load_x_seq=None):
            with ExitStack() as st:
                xbrot = Rot(st, "xb", [128, D], BF16, 2)
                wg = Rot(st, "wg", [128, 8, FG * 128], BF16, 2)
                wu = Rot(st, "wu", [128, 8, FG * 128], BF16, 2)
                wo = Rot(st, "wo", [128, D], BF16, 6)
                hrot = Rot(st, "h", [128, SEQ], BF16, 4)
                sgrot = Rot(st, "sg", [128, 512], BF16, 3)
                lnrot = Rot(st, "lnst", [128, 16], F32, 4)
                g_bc = sbt(st, "g_bc", [128, D], F32)
                b_bc = sbt(st, "b_bc", [128, D], F32)
                r_ln = S.res("lnp")
                load_ln(ln_idx, g_bc, b_bc, r_ln)
                prep_banks = BankRot([4, 5, 6, 7])
                for t in range(NT):
                    if load_x_seq is not None:
                        DMA("sp", acc[:, t, :], x_d[load_x_seq, t * 128:(t + 1) * 128, :], f"xin{t}", [], [r_acc[t]])
                    prep(t, xbrot, prep_banks)
                gate_banks = BankRot([0, 1])
                up_banks = BankRot([2, 3])
                y_banks = BankRot([4, 5, 6, 7])
                ngroups = NF // FG

                def load_group(gi):
                    f0 = gi * FG
                    wgt, r_wg, k_wg = wg.next()
                    wut, r_wu, k_wu = wu.next()
                    DMA("pool", wgt[:], wi_d[:, f0 * 128:(f0 + FG) * 128].rearrange("(kc p) f -> p kc f", p=128), k_wg, [], [r_wg])
                    DMA("pool", wut[:], wi_d[:, DFF + f0 * 128:DFF + (f0 + FG) * 128].rearrange("(kc p) f -> p kc f", p=128), k_wu, [], [r_wu])
                    wos = []
                    for j in range(FG):
                        wot, r_wo, k_wo = wo.next()
                        DMA("pool", wot[:], wo_d[(f0 + j) * 128:(f0 + j + 1) * 128, :], k_wo, [], [r_wo])
                        wos.append((wot, r_wo))
                    return (wgt, r_wg, wut, r_wu, wos)

                def first(W):
                    wgt, r_wg, wut, r_wu, wos = W
                    hs = []
                    for j in range(FG):
                        ht, r_h, _ = hrot.next()
                        hs.append((ht, r_h))
                        for tb in range(4):
                            gb, r_gb = gate_banks.next()
                            ub, r_ub = up_banks.next()
                            rx = [r_xT[tb * 4 + i] for i in range(4)]
                            for kc in range(8):
                                MM(gb[:], wgt[:, kc, j * 128:(j + 1) * 128], xT[:, kc, tb * 512:(tb + 1) * 512], kc == 0, kc == 7, [r_wg] + rx, [r_gb])
                            for kc in range(8):
                                MM(ub[:], wut[:, kc, j * 128:(j + 1) * 128], xT[:, kc, tb * 512:(tb + 1) * 512], kc == 0, kc == 7, [r_wu] + rx, [r_ub])
                            sg, r_sg, _ = sgrot.next()
                            ACT(sg[:], gb[:], AF.Silu, [r_gb], [r_sg])
                            TT("dve", ht[:, tb * 512:(tb + 1) * 512], ub[:], sg[:], ALU.mult, [r_ub, r_sg], [r_h])
                    return hs

                def second(W, hs, last):
                    wgt, r_wg, wut, r_wu, wos = W
                    for t in range(NT):
                        yb = [y_banks.next() for _ in range(2)]
                        for j in range(FG):
                            ht, r_h = hs[j]
                            wot, r_wo = wos[j]
                            for hf in range(2):
                                MM(yb[hf][0][:], ht[:, t * 128:(t + 1) * 128], wot[:, hf * 512:(hf + 1) * 512], j == 0, j == FG - 1,
                                   [r_h, r_wo], [yb[hf][1]])
                        for hf in range(2):
                            STT("dve", acc[:, t, hf * 512:(hf + 1) * 512], yb[hf][0][:], 0.5, acc[:, t, hf * 512:(hf + 1) * 512],
                                ALU.mult, ALU.add, [yb[hf][1], r_acc[t]], [r_acc[t]])
                        if last:
                            layernorm(t, lnrot, g_bc, b_bc, r_ln)

                prevW = None
                prevH = None
                for gi in range(ngroups):
                    W = load_group(gi)
                    hs = first(W)
                    if prevW is not None:
                        second(prevW, prevH, False)
                    prevW, prevH = W, hs
                second(prevW, prevH, True)
                S.barrier()

        def attn_phase():
            with ExitStack() as st:
                xbrot = Rot(st, "xb", [128, D], BF16, 2)
                wq = sbt(st, "wq", [128, 8, 768], BF16)
                r_wq = S.res("wq")
                cs = sbt(st, "cs", [128, NT, 32], F32)
                sn = sbt(st, "sn", [128, NT, 32], F32)
                r_cs = S.res("cs")
                qT = sbt(st, "qT", [128, 4, SEQ], BF16)
                r_qT = [S.res(f"qT{t}") for t in range(NT)]
                kT = sbt(st, "kT", [128, SEQ], BF16)
                r_kT = [S.res(f"kT{t}") for t in range(NT)]
                va = sbt(st, "va", [128, NT, 2, 128], BF16)
                r_va = [S.res(f"va{t}") for t in range(NT)]
                woa = sbt(st, "woa", [128, 4, D], BF16)
                r_woa = S.res("woa")
                attT = Rot(st, "attT", [128, 4, 512], BF16, 2)
                pt = Rot(st, "pt", [128, 512], BF16, 4)
                rec = Rot(st, "rec", [128, 512], F32, 1)
                qfr = Rot(st, "qf", [128, 640], F32, 2)
                sqr = Rot(st, "sq", [128, 640], F32, 1)
                ssr = Rot(st, "ssr", [128, 32], F32, 2)
                rar = Rot(st, "ra", [128, 320], F32, 2)
                rbr = Rot(st, "rb", [128, 320], F32, 2)
                qrr = Rot(st, "qr", [128, 4, 2, 64], BF16, 2)
                krr = Rot(st, "kr", [128, 128], BF16, 2)

                DMA("pool", wq[:], w_in[:, 0:768].rearrange("(kc p) f -> p kc f", p=128), "wq", [], [r_wq])
                DMA("sp", cs[:], c_cos.rearrange("(t p) f -> p t f", p=128), "cs", [], [r_cs])
                DMA("sp", sn[:], c_sin.rearrange("(t p) f -> p t f", p=128), "sn", [], [r_cs])
                for g in range(2):
                    for j in range(4):
                        hh = g * 4 + j
                        DMA("pool", woa[g * 64:(g + 1) * 64, j, :], w_out[hh * 64:(hh + 1) * 64, :], f"woa{(g * 4 + j) % 2}", [], [r_woa])
                MEMSET("pool", va[:, :, 0, 64:128], 1.0, r_va)
                MEMSET("pool", va[:, :, 1, 0:64], 1.0, r_va)

                prep_banks = BankRot([6, 7])
                q_banks = BankRot([0, 1])
                kv_banks = BankRot([2, 3])
                tr_banks = BankRot([4, 5])
                for t in range(NT):
                    prep(t, xbrot, prep_banks)
                for t in range(NT):
                    qb_, r_qb = q_banks.next()
                    kvb, r_kvb = kv_banks.next()
                    for kc in range(8):
                        MM(qb_[:], xT[:, kc, t * 128:(t + 1) * 128], wq[:, kc, 0:512], kc == 0, kc == 7, [r_xT[t], r_wq], [r_qb])
                    for kc in range(8):
                        MM(kvb[:, 0:256], xT[:, kc, t * 128:(t + 1) * 128], wq[:, kc, 512:768], kc == 0, kc == 7, [r_xT[t], r_wq], [r_kvb])
                    qf, r_qf, _ = qfr.next()
                    ACTcopy(qf[:, 0:512], qb_[:], [r_qb], [r_qf])
                    ACTcopy(qf[:, 512:640], kvb[:, 0:128], [r_kvb], [r_qf])
                    ACTcopy(va[:, t, 0, 0:64], kvb[:, 128:192], [r_kvb], [r_va[t]])
                    ACTcopy(va[:, t, 1, 64:128], kvb[:, 192:256], [r_kvb], [r_va[t]])
                    sq, r_sq, _ = sqr.next()
                    ss, r_ss, _ = ssr.next()
                    qf3 = qf[:].rearrange("p (h d) -> p h d", d=64)
                    TT("dve", sq[:], qf[:], qf[:], ALU.mult, [r_qf], [r_sq])
                    S.op("dve", "tensor_reduce", dict(out=ss[:, 0:10], in_=sq[:].rearrange("p (h d) -> p h d", d=64), axis=AX.X, op=ALU.add),
                         [r_sq], [r_ss])
                    ACT(ss[:, 10:20], ss[:, 0:10], AF.Ln, [r_ss], [r_ss], bias=RMS_EPS, scale=1.0 / 64)
                    ACT(ss[:, 20:30], ss[:, 10:20], AF.Exp, [r_ss], [r_ss], scale=-0.5)
                    TT("dve", qf3, qf3, ss[:, 20:30].unsqueeze(2).to_broadcast([128, 10, 64]), ALU.mult, [r_qf, r_ss], [r_qf])
                    TT("dve", qf3[:, 0:8, :], qf3[:, 0:8, :], nw[:, 0:1, :].to_broadcast([128, 8, 64]), ALU.mult, [r_qf, r_const], [r_qf])
                    TT("dve", qf3[:, 8:10, :], qf3[:, 8:10, :], nw[:, 1:2, :].to_broadcast([128, 2, 64]), ALU.mult, [r_qf, r_const], [r_qf])
                    x0 = qf3[:, :, 0::2]
                    x1 = qf3[:, :, 1::2]
                    cb = cs[:, t, :].unsqueeze(1).to_broadcast([128, 10, 32])
                    sb_ = sn[:, t, :].unsqueeze(1).to_broadcast([128, 10, 32])
                    ra, r_ra, _ = rar.next()
                    rb, r_rb, _ = rbr.next()
                    ra3 = ra[:].rearrange("p (h d) -> p h d", d=32)
                    rb3 = rb[:].rearrange("p (h d) -> p h d", d=32)
                    qr, r_qr, _ = qrr.next()
                    kr, r_kr, _ = krr.next()
                    qrv = qr[:].rearrange("p j g d -> p g j d")
                    kr3 = kr[:].rearrange("p (k d) -> p k d", d=64)
                    TT("dve", ra3, x0, cb, ALU.mult, [r_qf, r_cs], [r_ra])
                    TT("dve", rb3, x1, sb_, ALU.mult, [r_qf, r_cs], [r_rb])
                    TT("dve", qrv[:, :, :, 0::2], ra3[:, 0:8, :].rearrange("p (g j) d -> p g j d", g=2),
                       rb3[:, 0:8, :].rearrange("p (g j) d -> p g j d", g=2), ALU.subtract, [r_ra, r_rb], [r_qr])
                    TT("dve", kr3[:, :, 0::2], ra3[:, 8:10, :], rb3[:, 8:10, :], ALU.subtract, [r_ra, r_rb], [r_kr])
                    TT("dve", ra3, x0, sb_, ALU.mult, [r_qf, r_cs], [r_ra])
                    TT("dve", rb3, x1, cb, ALU.mult, [r_qf, r_cs], [r_rb])
                    TT("dve", qrv[:, :, :, 1::2], ra3[:, 0:8, :].rearrange("p (g j) d -> p g j d", g=2),
                       rb3[:, 0:8, :].rearrange("p (g j) d -> p g j d", g=2), ALU.add, [r_ra, r_rb], [r_qr])
                    TT("dve", kr3[:, :, 1::2], ra3[:, 8:10, :], rb3[:, 8:10, :], ALU.add, [r_ra, r_rb], [r_kr])
                    trb, r_trb = tr_banks.next()
                    trv = bf16view(trb, 8)
                    for j in range(4):
                        TR(trv[:, j, :], qr[:, j, :, :].rearrange("p g d -> p (g d)"), identb[:], [r_qr], [r_trb])
                    TR(trv[:, 4, :], kr[:], identb[:], [r_kr], [r_trb])
                    ACTcopy(qT[:, :, t * 128:(t + 1) * 128], trv[:, 0:4, :], [r_trb], [r_qT[t]])
                    ACTcopy(kT[:, t * 128:(t + 1) * 128], trv[:, 4, :], [r_trb], [r_kT[t]])

                sc_banks = BankRot([0, 1, 2, 3])
                ov_banks = BankRot([4, 5])
                o_banks = BankRot([6, 7])
                for qb in range(4):
                    at, r_at, _ = attT.next()
                    rq = [r_qT[qb * 4 + i] for i in range(4)]
                    for g in range(2):
                        vr = slice(g * 64, (g + 1) * 64)
                        sr = slice((1 - g) * 64, (2 - g) * 64)
                        for j in range(4):
                            ov, r_ov = ov_banks.next()
                            for kc in range(NT):
                                sc, r_sc = sc_banks.next()
                                MM(sc[:], kT[vr, kc * 128:(kc + 1) * 128], qT[vr, j, qb * 512:(qb + 1) * 512], True, True,
                                   [r_kT[kc]] + rq, [r_sc])
                                ptt, r_pt, _ = pt.next()
                                ACT(ptt[:], sc[:], AF.Exp, [r_sc], [r_pt], scale=0.125)
                                MM(ov[:], va[:, kc, g, :], ptt[:], kc == 0, kc == NT - 1, [r_va[kc], r_pt], [r_ov])
                            rc, r_rc, _ = rec.next()
                            S.op("dve", "reciprocal", dict(out=rc[vr, :], in_=ov[sr, :]), [r_ov], [r_rc])
                            TT("dve", at[vr, j, :], ov[vr, :], rc[vr, :], ALU.mult, [r_ov, r_rc], [r_at])
                    for tt in range(4):
                        t = qb * 4 + tt
                        yb = [o_banks.next() for _ in range(2)]
                        for hf in range(2):
                            for j in range(4):
                                MM(yb[hf][0][:], at[:, j, tt * 128:(tt + 1) * 128], woa[:, j, hf * 512:(hf + 1) * 512], j == 0, j == 3,
                                   [r_at, r_woa], [yb[hf][1]])
                        for hf in range(2):
                            TT("dve", acc[:, t, hf * 512:(hf + 1) * 512], yb[hf][0][:], acc[:, t, hf * 512:(hf + 1) * 512], ALU.add,
                               [yb[hf][1], r_acc[t]], [r_acc[t]])
                S.barrier()

        def ssd_phase():
            with ExitStack() as st:
                allb = BankRot([0, 1, 2, 3, 4, 5, 6, 7])
                wdt = sbt(st, "wdt", [128, 8, 16], BF16)
                r_wdt = S.res("wdt")
                dtall = sbt(st, "dtall", [128, NT, 16], F32)
                aall = sbt(st, "aall", [128, NT, 16], F32)
                Eall = sbt(st, "Eall", [128, NT, 48], F32)
                sfd = sbt(st, "sfd", [128, NT, 8], F32)
                sbd = sbt(st, "sbd", [128, NT, 8], F32)
                r_dt = S.res("dt")
                wg4 = sbt(st, "wg4", [128, 8, 512], BF16)
                r_wg4 = S.res("wg4")
                wz = sbt(st, "wz", [128, 8, 256], BF16)
                r_wz = S.res("wz")
                wos = sbt(st, "wos", [128, 2, D], BF16)
                r_wos = S.res("wos")
                pre = sbt(st, "pre", [128, SEQ + 4], F32)
                r_pre = S.res("pre")
                xsT = sbt(st, "xsT", [128, 2, SEQ], F32)
                r_xsT = [S.res("xsT0"), S.res("xsT1")]
                BT = sbt(st, "BT", [128, SEQ], BF16)
                CT = sbt(st, "CT", [128, SEQ], BF16)
                r_BT = S.res("BT")
                r_CT = S.res("CT")
                hbst = sbt(st, "hbst", [128, NT, 256], BF16)
                r_hbst = [S.res(f"hbst{c}") for c in range(NT)]
                Hf = sbt(st, "Hf", [128, 256], F32)
                Hb = sbt(st, "Hb", [128, 256], F32)
                r_Hf = S.res("Hf")
                r_Hb = S.res("Hb")
                xdf_r = Rot(st, "xdf", [128, 256], BF16, 2)
                xdb_r = Rot(st, "xdb", [128, 256], BF16, 2)
                xdd_r = Rot(st, "xdd", [128, 256], BF16, 2)
                xD_r = Rot(st, "xD", [128, 256], F32, 2)
                btm_r = Rot(st, "btm", [128, 128], BF16, 2)
                cbf_r = Rot(st, "cbf", [128, 128], F32, 2)
                cbb_r = Rot(st, "cbb", [128, 128], F32, 2)
                R_r = Rot(st, "R", [128, 512], F32, 1)
                E_r = Rot(st, "E", [128, 512], F32, 1)
                Mf_r = Rot(st, "Mf", [128, 512], BF16, 2)
                Mb_r = Rot(st, "Mb", [128, 512], BF16, 2)
                y_r = Rot(st, "y", [128, 256], F32, 1)
                t1_r = Rot(st, "t1", [128, 256], F32, 1)
                sz_r = Rot(st, "sz", [128, 256], F32, 1)
                yg_r = Rot(st, "yg", [128, 256], F32, 1)
                junk_r = Rot(st, "junk", [128, 256], F32, 1)
                yn_r = Rot(st, "yn", [128, 256], BF16, 2)
                ynT_r = Rot(st, "ynT", [128, 2, 128], BF16, 2)
                hfb_r = Rot(st, "hfb", [128, 256], BF16, 2)
                ssq_r = Rot(st, "ssq", [128, 4], F32, 2)

                DMA("pool", wdt[:], w_in[:, C_DT:C_DT + 16].rearrange("(kc p) f -> p kc f", p=128), "wdt", [], [r_wdt])
                bk, r_bk = allb.next()
                for t in range(NT):
                    for kc in range(8):
                        MM(bk[:, t * 16:(t + 1) * 16], xT[:, kc, t * 128:(t + 1) * 128], wdt[:, kc, :], kc == 0, kc == 7, [r_xT[t], r_wdt], [r_bk])
                TT("dve", dtall[:], bk[:, 0:256].rearrange("p (t f) -> p t f", f=16), dtb_bc[:].unsqueeze(1).to_broadcast([128, NT, 16]),
                   ALU.add, [r_bk, r_const], [r_dt])
                ACT(dtall[:], dtall[:], AF.Exp, [r_dt], [r_dt])
                ACT(dtall[:], dtall[:], AF.Ln, [r_dt], [r_dt], bias=1.0)
                TT("dve", aall[:], dtall[:], A_bc[:].unsqueeze(1).to_broadcast([128, NT, 16]), ALU.mult, [r_dt, r_const], [r_dt])
                for half in range(2):
                    bk, r_bk = allb.next()
                    for c8 in range(8):
                        c = half * 8 + c8
                        o0 = c8 * 48
                        MM(bk[:, o0:o0 + 8], M_LE, aall[:, c, 0:8], True, True, [r_dt, r_const], [r_bk])
                        MM(bk[:, o0 + 8:o0 + 16], M_GT, aall[:, c, 0:8], True, True, [r_dt, r_const], [r_bk])
                        MM(bk[:, o0 + 16:o0 + 24], M_LT, aall[:, c, 8:16], True, True, [r_dt, r_const], [r_bk])
                        MM(bk[:, o0 + 24:o0 + 32], M_GE, aall[:, c, 8:16], True, True, [r_dt, r_const], [r_bk])
                        MM(bk[:, o0 + 32:o0 + 48], onesf[:], aall[:, c, :], True, True, [r_dt, r_const], [r_bk])
                    ACT(Eall[:, half * 8:(half + 1) * 8, :], bk[:, 0:384].rearrange("p (c f) -> p c f", f=48), AF.Exp, [r_bk], [r_dt])
                TT("dve", sfd[:], dtall[:, :, 0:8], Eall[:, :, 8:16], ALU.mult, [r_dt], [r_dt])
                TT("dve", sbd[:], dtall[:, :, 8:16], Eall[:, :, 16:24], ALU.mult, [r_dt], [r_dt])

                MEMSET("pool", pre[:, 0:2], 0.0, [r_pre])
                MEMSET("pool", pre[:, SEQ + 2:SEQ + 4], 0.0, [r_pre])
                if SSD_STOP[0] == 1:
                    S.barrier()
                    return

                def bc4(ap4):
                    return ap4.unsqueeze(2).to_broadcast([128, 4, 64])

                def v4(ap):
                    return ap.rearrange("p (r d) -> p r d", d=64)

                for g in range(2):
                    DMA("pool", wg4[:, :, 0:256], w_in[:, C_XS + g * 256:C_XS + (g + 1) * 256].rearrange("(kc p) f -> p kc f", p=128), "wg4a", [], [r_wg4])
                    DMA("pool", wg4[:, :, 256:384], w_in[:, C_B + g * 128:C_B + (g + 1) * 128].rearrange("(kc p) f -> p kc f", p=128), "wg4b", [], [r_wg4])
                    DMA("pool", wg4[:, :, 384:512], w_in[:, C_C + g * 128:C_C + (g + 1) * 128].rearrange("(kc p) f -> p kc f", p=128), "wg4c", [], [r_wg4])
                    DMA("pool", wz[:], w_in[:, C_Z + g * 256:C_Z + (g + 1) * 256].rearrange("(kc p) f -> p kc f", p=128), "wz", [], [r_wz])
                    for j in range(2):
                        r0 = 512 + g * 256 + j * 128
                        DMA("pool", wos[:, j, :], w_out[r0:r0 + 128, :], f"wos{j}", [], [r_wos])
                    plan = [(2, 4 + g, 0, "B"), (3, 6 + g, 1, "C"), (0, g * 2, 0, "x"), (1, g * 2 + 1, 1, "x")]
                    for (cc, ch, slot, kind) in plan:
                        for tb in range(4):
                            bk, r_bk = allb.next()
                            rx = [r_xT[tb * 4 + i] for i in range(4)]
                            for kc in range(8):
                                MM(bk[:], wg4[:, kc, cc * 128:(cc + 1) * 128], xT[:, kc, tb * 512:(tb + 1) * 512], kc == 0, kc == 7, [r_wg4] + rx, [r_bk])
                            ACTcopy(pre[:, 2 + tb * 512:2 + (tb + 1) * 512], bk[:], [r_bk], [r_pre])
                        eng = "dve"
                        o = xsT[:, slot, :]
                        r_o = r_xsT[slot]
                        TS(eng, o, pre[:, 0:SEQ], cwall[:, ch, 0:1], None, ALU.mult, None, [r_pre, r_const], [r_o])
                        for jj in range(1, 5):
                            STT(eng, o, pre[:, jj:jj + SEQ], cwall[:, ch, jj:jj + 1], o, ALU.mult, ALU.add, [r_pre, r_const, r_o], [r_o])
                        if kind == "B":
                            ACT(BT[:], o, AF.Silu, [r_o, r_const], [r_BT], bias=cball[:, ch:ch + 1])
                        elif kind == "C":
                            ACT(CT[:], o, AF.Silu, [r_o, r_const], [r_CT], bias=cball[:, ch:ch + 1])
                        else:
                            ACT(o, o, AF.Silu, [r_o, r_const], [r_o], bias=cball[:, ch:ch + 1])

                    if SSD_STOP[0] == 2:
                        S.barrier()
                        return

                    def chunk_common(c, need_b):
                        bx, r_bx = allb.next()
                        for cc in range(2):
                            TR(bx[:, cc * 128:(cc + 1) * 128], xsT[:, cc, c * 128:(c + 1) * 128], identf[:], [r_xsT[cc]], [r_bx])
                        btm, r_btm = None, None
                        if need_b:
                            bb, r_bb = allb.next()
                            bbv = bf16view(bb, 8)
                            TR(bbv[:, 0, :], BT[:, c * 128:(c + 1) * 128], identb[:], [r_BT], [r_bb])
                            btm, r_btm, _ = btm_r.next()
                            ACTcopy(btm[:], bbv[:, 0, :], [r_bb], [r_btm])
                        return bx, r_bx, btm, r_btm

                    MEMSET("pool", Hb[:], 0.0, [r_Hb])
                    for c in range(NT - 1, -1, -1):
                        ACTcopy(hbst[:, c, :], Hb[:], [r_Hb], [r_hbst[c]])
                        if c == 0:
                            break
                        bx, r_bx, btm, r_btm = chunk_common(c, True)
                        xdd, r_xdd, _ = xdd_r.next()
                        TT("dve", v4(xdd[:]), v4(bx[:, 0:256]), bc4(sbd[:, c, g * 4:g * 4 + 4]), ALU.mult, [r_bx, r_dt], [r_xdd])
                        bs, r_bs = allb.next()
                        MM(bs[:, 0:256], btm[:], xdd[:], True, True, [r_btm, r_xdd], [r_bs])
                        TT("dve", v4(Hb[:]), v4(Hb[:]), bc4(Eall[:, c, 40 + g * 4:44 + g * 4]), ALU.mult, [r_Hb, r_dt], [r_Hb])
                        TT("dve", Hb[:], bs[:, 0:256], Hb[:], ALU.add, [r_bs, r_Hb], [r_Hb])

                    if SSD_STOP[0] == 3:
                        S.barrier()
                        return
                    MEMSET("pool", Hf[:], 0.0, [r_Hf])
                    for c in range(NT):
                        bx, r_bx, btm, r_btm = chunk_common(c, True)
                        xdf, r_xdf, _ = xdf_r.next()
                        xdb, r_xdb, _ = xdb_r.next()
                        xdd, r_xdd, _ = xdd_r.next()
                        xD, r_xD, _ = xD_r.next()
                        xt4 = v4(bx[:, 0:256])
                        TT("dve", v4(xdf[:]), xt4, bc4(dtall[:, c, g * 4:g * 4 + 4]), ALU.mult, [r_bx, r_dt], [r_xdf])
                        TT("dve", v4(xdb[:]), xt4, bc4(dtall[:, c, 8 + g * 4:12 + g * 4]), ALU.mult, [r_bx, r_dt], [r_xdb])
                        TT("dve", v4(xdd[:]), xt4, bc4(sfd[:, c, g * 4:g * 4 + 4]), ALU.mult, [r_bx, r_dt], [r_xdd])
                        TT("dve", v4(xD[:]), xt4, bc4(dsk_bc[:, g * 4:g * 4 + 4]), ALU.mult, [r_bx, r_const], [r_xD])
                        bc_, r_bc = allb.next()
                        MM(bc_[:, 0:128], BT[:, c * 128:(c + 1) * 128], CT[:, c * 128:(c + 1) * 128], True, True, [r_BT, r_CT], [r_bc])
                        cbf, r_cbf, _ = cbf_r.next()
                        cbb, r_cbb, _ = cbb_r.next()
                        TT("dve", cbf[:], bc_[:, 0:128], M_LE, ALU.mult, [r_bc, r_const], [r_cbf])
                        TT("dve", cbb[:], bc_[:, 0:128], M_GE, ALU.mult, [r_bc, r_const], [r_cbb])
                        Ms = []
                        for (dr, mask_r, mask_l, cbx, r_cbx, Mrot) in ((0, M_LE, M_GT, cbf, r_cbf, Mf_r), (1, M_GE, M_LT, cbb, r_cbb, Mb_r)):
                            Rt, r_R, _ = R_r.next()
                            R3 = Rt[:].rearrange("p (r l) -> p r l", l=128)
                            a4 = aall[:, c, dr * 8 + g * 4:dr * 8 + g * 4 + 4]
                            TT("dve", R3, mask_r.unsqueeze(1).to_broadcast([128, 4, 128]), a4.unsqueeze(2).to_broadcast([128, 4, 128]), ALU.mult,
                               [r_const, r_dt], [r_R])
                            ba, r_ba = allb.next()
                            MM(ba[:], mask_l, Rt[:], True, True, [r_const, r_R], [r_ba])
                            Et, r_E, _ = E_r.next()
                            ACT(Et[:], ba[:], AF.Exp, [r_ba], [r_E])
                            Mt, r_M, _ = Mrot.next()
                            TT("dve", Mt[:].rearrange("p (r l) -> p r l", l=128), Et[:].rearrange("p (r l) -> p r l", l=128),
                               cbx[:].unsqueeze(1).to_broadcast([128, 4, 128]), ALU.mult, [r_E, r_cbx], [r_M])
                            Ms.append((Mt, r_M))
                        yd, r_yd = allb.next()
                        for r in range(4):
                            MM(yd[:, r * 64:(r + 1) * 64], Ms[0][0][:, r * 128:(r + 1) * 128], xdf[:, r * 64:(r + 1) * 64], True, False,
                               [Ms[0][1], r_xdf], [r_yd])
                            MM(yd[:, r * 64:(r + 1) * 64], Ms[1][0][:, r * 128:(r + 1) * 128], xdb[:, r * 64:(r + 1) * 64], False, True,
                               [Ms[1][1], r_xdb], [r_yd])
                        hfb, r_hfb, _ = hfb_r.next()
                        ACTcopy(hfb[:], Hf[:], [r_Hf], [r_hfb])
                        bf_, r_bf = allb.next()
                        MM(bf_[:, 0:256], CT[:, c * 128:(c + 1) * 128], hfb[:], True, True, [r_CT, r_hfb], [r_bf])
                        bb2, r_bb2 = allb.next()
                        MM(bb2[:, 0:256], CT[:, c * 128:(c + 1) * 128], hbst[:, c, :], True, True, [r_CT, r_hbst[c]], [r_bb2])
                        if c < NT - 1:
                            bs, r_bs = allb.next()
                            MM(bs[:, 0:256], btm[:], xdd[:], True, True, [r_btm, r_xdd], [r_bs])
                            TT("dve", v4(Hf[:]), v4(Hf[:]), bc4(Eall[:, c, 32 + g * 4:36 + g * 4]), ALU.mult, [r_Hf, r_dt], [r_Hf])
                            TT("dve", Hf[:], bs[:, 0:256], Hf[:], ALU.add, [r_bs, r_Hf], [r_Hf])
                        y, r_y, _ = y_r.next()
                        t1, r_t1, _ = t1_r.next()
                        TT("dve", y[:], yd[:, 0:256], xD[:], ALU.add, [r_yd, r_xD], [r_y])
                        TT("dve", v4(t1[:]), v4(bf_[:, 0:256]), bc4(Eall[:, c, g * 4:g * 4 + 4]), ALU.mult, [r_bf, r_dt], [r_t1])
                        TT("dve", y[:], y[:], t1[:], ALU.add, [r_y, r_t1], [r_y])
                        TT("dve", v4(t1[:]), v4(bb2[:, 0:256]), bc4(Eall[:, c, 24 + g * 4:28 + g * 4]), ALU.mult, [r_bb2, r_dt], [r_t1])
                        TT("dve", y[:], y[:], t1[:], ALU.add, [r_y, r_t1], [r_y])
                        if dbg == "yssd":
                            DMA("sp", dbg_d[c * 128:(c + 1) * 128, g * 256:(g + 1) * 256], y[:], f"dbg{c % 4}", [r_y], [])
                        bz, r_bz = allb.next()
                        for kc in range(8):
                            MM(bz[:, 0:256], xT[:, kc, c * 128:(c + 1) * 128], wz[:, kc, :], kc == 0, kc == 7, [r_xT[c], r_wz], [r_bz])
                        sz, r_sz, _ = sz_r.next()
                        ACT(sz[:], bz[:, 0:256], AF.Silu, [r_bz], [r_sz])
                        yg, r_yg, _ = yg_r.next()
                        TT("dve", yg[:], y[:], sz[:], ALU.mult, [r_y, r_sz], [r_yg])
                        ssq, r_ssq, _ = ssq_r.next()
                        junk, r_junk, _ = junk_r.next()
                        MEMSET("pool", ssq[:], 0.0, [r_ssq])
                        ACT(junk[:], yg[:], AF.Square, [r_yg, r_ssq], [r_junk, r_ssq], accum_out=ssq[:, 0:1])
                        ACT(ssq[:, 1:2], ssq[:, 0:1], AF.Ln, [r_ssq], [r_ssq], bias=RMS_EPS, scale=1.0 / 256)
                        ACT(ssq[:, 2:3], ssq[:, 1:2], AF.Exp, [r_ssq], [r_ssq], scale=-0.5)
                        yn, r_yn, _ = yn_r.next()
                        STT("dve", yn[:], yg[:], ssq[:, 2:3], snw[:, g * 256:(g + 1) * 256], ALU.mult, ALU.mult, [r_yg, r_ssq, r_const], [r_yn])
                        if dbg == "ssd":
                            dtmp, r_dtmp, _ = junk_r.next()
                            CP("dve", dtmp[:], yn[:], [r_yn], [r_dtmp])
                            DMA("sp", dbg_d[c * 128:(c + 1) * 128, g * 256:(g + 1) * 256], dtmp[:], f"dbg{c % 4}", [r_dtmp], [])
                        bt_, r_bt = allb.next()
                        btv = bf16view(bt_, 8)
                        for j in range(2):
                            TR(btv[:, j, :], yn[:, j * 128:(j + 1) * 128], identb[:], [r_yn], [r_bt])
                        ynT, r_ynT, _ = ynT_r.next()
                        ACTcopy(ynT[:], btv[:, 0:2, :], [r_bt], [r_ynT])
                        for hf in range(2):
                            bo, r_bo = allb.next()
                            for j in range(2):
                                MM(bo[:], ynT[:, j, :], wos[:, j, hf * 512:(hf + 1) * 512], j == 0, j == 1, [r_ynT, r_wos], [r_bo])
                            TT("dve", acc[:, c, hf * 512:(hf + 1) * 512], bo[:], acc[:, c, hf * 512:(hf + 1) * 512], ALU.add,
                               [r_bo, r_acc[c]], [r_acc[c]])
                S.barrier()

        def ln_phase(idx):
            with ExitStack() as st:
                lnrot = Rot(st, "lnst", [128, 16], F32, 4)
                g_bc = sbt(st, "g_bc", [128, D], F32)
                b_bc = sbt(st, "b_bc", [128, D], F32)
                r_ln = S.res("lnp")
                load_ln(idx, g_bc, b_bc, r_ln)
                for t in range(NT):
                    layernorm(t, lnrot, g_bc, b_bc, r_ln)
                S.barrier()

        def ple_phase(s):
            with ExitStack() as st:
                xbrot = Rot(st, "xb", [128, D], BF16, 2)
                wgate = sbt(st, "wgate", [128, 8, D], BF16)
                wple = sbt(st, "wple", [128, 2, D], BF16)
                gb_bc = sbt(st, "gb_bc", [128, D], F32)
                r_w = S.res("plew")
                lnrot = Rot(st, "lnst", [128, 16], F32, 4)
                g_bc = sbt(st, "g_bc", [128, D], F32)
                b_bc = sbt(st, "b_bc", [128, D], F32)
                r_ln = S.res("lnp")
                pin_r = Rot(st, "pin", [128, 256], F32, 2)
                pb_r = Rot(st, "pb", [128, 256], BF16, 2)
                pT_r = Rot(st, "pT", [128, 2, 128], BF16, 2)
                gt_r = Rot(st, "gt", [128, 512], F32, 2)
                load_ln(3, g_bc, b_bc, r_ln)
                for h2 in range(2):
                    DMA("pool", wgate[:, :, h2 * 512:(h2 + 1) * 512], ple_gw[:, h2 * 512:(h2 + 1) * 512].rearrange("(kc p) f -> p kc f", p=128),
                        f"wgate{h2}", [], [r_w])
                DMA("pool", wple[:], ple_w.rearrange("(kc p) f -> p kc f", p=128), "wple", [], [r_w])
                DMA("sp", gb_bc[:], ple_gb.partition_broadcast(128), "gbb", [], [r_w])
                prep_banks = BankRot([6, 7])
                pt_banks = BankRot([4, 5])
                g_banks = BankRot([0, 1])
                e_banks = BankRot([2, 3])
                for t in range(NT):
                    prep(t, xbrot, prep_banks)
                    pin, r_pin, k_pin = pin_r.next()
                    DMA("sp", pin[:], p_d[s, t * 128:(t + 1) * 128, :], k_pin, [], [r_pin])
                    pb, r_pb, _ = pb_r.next()
                    CP("dve", pb[:], pin[:], [r_pin], [r_pb])
                    ptb, r_ptb = pt_banks.next()
                    ptv = bf16view(ptb, 8)
                    for kc in range(2):
                        TR(ptv[:, kc, :], pb[:, kc * 128:(kc + 1) * 128], identb[:], [r_pb], [r_ptb])
                    pT, r_pT, _ = pT_r.next()
                    ACTcopy(pT[:], ptv[:, 0:2, :], [r_ptb], [r_pT])
                    for hf in range(2):
                        gbk, r_gbk = g_banks.next()
                        ebk, r_ebk = e_banks.next()
                        for kc in range(8):
                            MM(gbk[:], xT[:, kc, t * 128:(t + 1) * 128], wgate[:, kc, hf * 512:(hf + 1) * 512], kc == 0, kc == 7, [r_xT[t], r_w], [r_gbk])
                        for kc in range(2):
                            MM(ebk[:], pT[:, kc, :], wple[:, kc, hf * 512:(hf + 1) * 512], kc == 0, kc == 1, [r_pT, r_w], [r_ebk])
                        gt, r_gt, _ = gt_r.next()
                        TT("dve", gt[:], gbk[:], gb_bc[:, hf * 512:(hf + 1) * 512], ALU.add, [r_gbk, r_w], [r_gt])
                        ACT(gt[:], gt[:], AF.Sigmoid, [r_gt], [r_gt])
                        TT("dve", gt[:], ebk[:], gt[:], ALU.mult, [r_ebk, r_gt], [r_gt])
                        TT("pool", acc[:, t, hf * 512:(hf + 1) * 512], acc[:, t, hf * 512:(hf + 1) * 512], gt[:], ALU.add, [r_acc[t], r_gt], [r_acc[t]])
                    layernorm(t, lnrot, g_bc, b_bc, r_ln)
                    DMA("sp", out_d[s, t * 128:(t + 1) * 128, :], acc[:, t, :], f"out{t}", [r_acc[t]], [])
                S.barrier()

        def dump_acc():
            for t in range(NT):
                DMA("sp", dbg_d[t * 128:(t + 1) * 128, :], acc[:, t, :], f"dbg{t % 4}", [r_acc[t]], [])

        def write_out(s):
            for t in range(NT):
                DMA("sp", out_d[s, t * 128:(t + 1) * 128, :], acc[:, t, :], f"out{t}", [r_acc[t]], [])
            S.barrier()

        for s in range(nseq):
            if stage == 30:
                with ExitStack() as st0:
                    xbrot0 = Rot(st0, "xb", [128, D], BF16, 2)
                    pb0 = BankRot([6, 7])
                    for t in range(NT):
                        DMA("sp", acc[:, t, :], x_d[s, t * 128:(t + 1) * 128, :], f"xin{t}", [], [r_acc[t]])
                        prep(t, xbrot0, pb0)
                    S.barrier()
                ssd_phase()
                if dbg == "acc":
                    dump_acc()
                write_out(s)
                continue
            ffn_phase(w1i, w1o, 0, load_x_seq=s)
            if stage == 1:
                if dbg == "acc" and s == 0:
                    dump_acc()
                write_out(s)
                continue
            attn_phase()
            if stage == 2:
                if dbg == "acc" and s == 0:
                    dump_acc()
                write_out(s)
                continue
            ssd_phase()
            if stage == 3:
                if dbg == "acc" and s == 0:
                    dump_acc()
                write_out(s)
                continue
            ln_phase(1)
            if stage == 4:
                if dbg == "acc" and s == 0:
                    dump_acc()
                write_out(s)
                continue
            ffn_phase(w2i, w2o, 2)
            if stage == 5:
                if dbg == "acc" and s == 0:
                    dump_acc()
                write_out(s)
                continue
            ple_phase(s)

        S.emit()
        print("sched stats", S.stats)
    return nc


def make_consts():
    i = np.arange(128)[:, None]
    j = np.arange(128)[None, :]
    masks = np.stack([(i <= j), (i >= j), (i > j), (i < j)]).astype(np.float32)
    rows = SEQ // 64
    row = np.repeat(np.arange(rows, dtype=np.float32), 64)
    col = np.tile(np.arange(64, dtype=np.float32), rows)
    inv = (10000.0 ** (-np.arange(0, 32, 2, dtype=np.float32) / 32)).astype(np.float32)
    ang = np.concatenate([row[:, None] * inv, col[:, None] * inv], axis=-1).astype(np.float32)
    return {"c_ident": np.eye(128, dtype=np.float32), "c_masks": masks,
            "c_cos": np.cos(ang).astype(np.float32), "c_sin": np.sin(ang).astype(np.float32)}


def make_in_maps(inputs, ncores, nseq):
    f = lambda a: np.ascontiguousarray(np.asarray(a, dtype=np.float32))
    shared = {
        "ln_g": f(inputs["ln_g"][0]), "ln_b": f(inputs["ln_b"][0]),
        "ffn1_w_in": f(inputs["ffn1_w_in"][0]), "ffn1_w_out": f(inputs["ffn1_w_out"][0]),
        "w_in": f(inputs["w_in"][0]), "q_norm": f(inputs["q_norm"][0]), "k_norm": f(inputs["k_norm"][0]),
        "conv_w": f(inputs["conv_w"][0]), "conv_b": f(inputs["conv_b"][0]),
        "dt_bias": f(inputs["dt_bias"][0]).reshape(16), "a_log": f(inputs["a_log"][0]).reshape(16),
        "d_skip": f(inputs["d_skip"][0]), "ssd_norm": f(inputs["ssd_norm"][0]),
        "w_out": f(inputs["w_out"][0]), "ffn2_w_in": f(inputs["ffn2_w_in"][0]), "ffn2_w_out": f(inputs["ffn2_w_out"][0]),
        "ple_w": f(inputs["ple_w"][0]), "ple_gate_w": f(inputs["ple_gate_w"][0]), "ple_gate_b": f(inputs["ple_gate_b"][0]),
    }
    shared.update(make_consts())
    x = np.asarray(inputs["x"], dtype=np.float32)
    p = np.asarray(inputs["p"], dtype=np.float32)[0]
    maps = []
    for c in range(ncores):
        m = dict(shared)
        m["x"] = np.ascontiguousarray(x[c * nseq:(c + 1) * nseq])
        m["p"] = np.ascontiguousarray(p[c * nseq:(c + 1) * nseq])
        maps.append(m)
    return maps


def kernel(**inputs):
    ncores, nseq = 8, 2
    nc = build_nc(nseq=nseq)
    maps = make_in_maps(inputs, ncores, nseq)
    res = run_bass_kernel_spmd(nc, maps, core_ids=list(range(ncores)))
    out = np.concatenate([np.asarray(r["out"]) for r in res.results], axis=0)
    return out.astype(np.float32)
```

```python
import numpy as np
import concourse.bass as bass
import concourse.mybir as mybir
from concourse.bass_utils import run_bass_kernel_spmd
from contextlib import ExitStack

F32 = mybir.dt.float32
BF16 = mybir.dt.bfloat16
AF = mybir.ActivationFunctionType
ALU = mybir.AluOpType
AX = mybir.AxisListType

ENGS = ("pe", "act", "dve", "pool", "sp")


class Res:
    __slots__ = ("name", "writer", "readers", "excl")

    def __init__(self, name, excl=False):
        self.name = name
        self.writer = None
        self.readers = []
        self.excl = excl


class Op:
    __slots__ = ("idx", "eng", "fn", "deps", "signal", "sem", "count", "is_dma", "known")

    def __init__(self, idx, eng, fn, is_dma, sem):
        self.idx = idx
        self.eng = eng
        self.fn = fn
        self.deps = []
        self.signal = False
        self.sem = sem
        self.count = None
        self.is_dma = is_dma
        self.known = None


class Sched:
    def __init__(self, nc, stack):
        self.nc = nc
        self.stack = stack
        self.ops = []
        self.eng_sem = {e: stack.enter_context(nc.semaphore("sem_" + e)) for e in ENGS}
        self.dma_sems = {}
        self.dma_counts = {}
        self.nres = 0
        self.last_real = {}
        self.dmas_since_barrier = []
        self.cap = None
        self.cap_stages = None

    def res(self, name=None, excl=False):
        self.nres += 1
        return Res(name or f"r{self.nres}", excl)

    def _dma_sem(self, key):
        if key not in self.dma_sems:
            self.dma_sems[key] = self.stack.enter_context(self.nc.semaphore("dsem_" + str(key)))
            self.dma_counts[key] = 0
        return self.dma_sems[key]

    def begin_task(self):
        self.cap = []
        self.cap_stages = [self.cap]

    def next_stage(self):
        self.cap = []
        self.cap_stages.append(self.cap)

    def end_task(self):
        st = self.cap_stages
        self.cap = None
        self.cap_stages = None
        return st

    def replay_skewed(self, tasks):
        K = max(len(t) for t in tasks)
        T = len(tasks)
        for step in range(T + K - 1):
            for k in range(K - 1, -1, -1):
                t = step - k
                if 0 <= t < T and k < len(tasks[t]):
                    for a in tasks[t][k]:
                        self.op(*a)

    def op(self, eng, meth, kwargs, reads=(), writes=(), dma_key=None):
        if self.cap is not None:
            self.cap.append((eng, meth, kwargs, tuple(reads), tuple(writes), dma_key))
            return None
        fn = (meth, kwargs)
        is_dma = dma_key is not None
        sem = self._dma_sem(dma_key) if is_dma else self.eng_sem[eng]
        o = Op(len(self.ops), eng, fn, is_dma, sem)
        if is_dma:
            self.dma_counts[dma_key] += 16
            o.count = self.dma_counts[dma_key]
            o.signal = True
            self.dmas_since_barrier.append(o)
        deps = {}
        wr = list(writes) + [r for r in reads if r.excl]
        for r in reads:
            if r.excl:
                continue
            if r.writer is not None:
                deps[r.writer.idx] = r.writer
        for w in wr:
            if w.writer is not None:
                deps[w.writer.idx] = w.writer
            for rd in w.readers:
                deps[rd.idx] = rd
        for r in reads:
            if not r.excl:
                if not is_dma:
                    r.readers = [x for x in r.readers if x.is_dma or x.eng != eng]
                r.readers.append(o)
        for w in wr:
            w.writer = o
            w.readers = []
        for p in deps.values():
            if p is o:
                continue
            if p.eng == "pe" and eng == "pe" and not p.is_dma and not is_dma:
                continue
            o.deps.append(p)
            p.signal = True
        self.ops.append(o)
        self.last_real[eng] = o
        return o

    def barrier(self):
        targets = [p for p in self.last_real.values()] + list(self.dmas_since_barrier)
        seen = set()
        tl = []
        for p in targets:
            if p.idx not in seen:
                seen.add(p.idx)
                tl.append(p)
        for e in ENGS:
            o = Op(len(self.ops), e, None, False, self.eng_sem[e])
            for p in tl:
                o.deps.append(p)
                p.signal = True
            self.ops.append(o)
        self.dmas_since_barrier = []

    def emit(self):
        nc = self.nc
        cnt = {e: 0 for e in ENGS}
        for o in self.ops:
            if not o.is_dma and o.signal and o.fn is not None:
                cnt[o.eng] += 1
                o.count = cnt[o.eng]
        known = {e: {} for e in ENGS}
        waits = {}
        for o in self.ops:
            k = known[o.eng]
            w = []
            for p in sorted(o.deps, key=lambda p: -p.idx):
                if k.get(p.sem, 0) >= p.count:
                    continue
                w.append((p.sem, p.count))
                k[p.sem] = p.count
                if p.known:
                    for s, c in p.known.items():
                        if k.get(s, 0) < c:
                            k[s] = c
            waits[o.idx] = w
            o.known = dict(k)
        per_eng = {e: [o for o in self.ops if o.eng == e] for e in ENGS}
        self.stats = {e: len(per_eng[e]) for e in ENGS}
        self.stats["waits"] = sum(len(w) for w in waits.values())
        self.stats["sems"] = len(self.dma_sems) + len(ENGS)
        self.stats["cnt"] = dict(cnt)

        def run(engname, eobj):
            for o in per_eng[engname]:
                for (s, c) in waits[o.idx]:
                    eobj.wait_ge(s, c)
                if o.fn is None:
                    continue
                ins = getattr(eobj, o.fn[0])(**o.fn[1])
                if o.is_dma:
                    ins.then_inc(o.sem, 16)
                elif o.signal:
                    ins.then_inc(o.sem, 1)

        with nc.Block() as block:
            @block.tensor
            def _(e):
                run("pe", e)

            @block.scalar
            def _(e):
                run("act", e)

            @block.vector
            def _(e):
                run("dve", e)

            @block.gpsimd
            def _(e):
                run("pool", e)

            @block.sync
            def _(e):
                run("sp", e)
                for key, s in self.dma_sems.items():
                    e.wait_ge(s, self.dma_counts[key])
                for en in ENGS:
                    if en != "sp" and cnt[en] > 0:
                        e.wait_ge(self.eng_sem[en], cnt[en])


D = 1024
SEQ = 2048
NT = 16
DFF = 2816
NF = 22
FG = 2
ALPHA = 2.0 ** 0.25
LN_EPS = 1e-5
RMS_EPS = 1e-6
SSD_STOP = [99]
C_Q, C_K, C_V, C_Z, C_XS, C_B, C_C, C_DT = 0, 512, 640, 768, 1280, 1792, 2048, 2304


def build_nc(nseq=2, stage=99, dbg=None):
    nc = bass.Bass("TRN2", target_bir_lowering=False)

    def din(name, shape):
        return nc.dram_tensor(name, list(shape), F32, kind="ExternalInput").ap()

    x_d = din("x", [nseq, SEQ, D])
    p_d = din("p", [nseq, SEQ, 256])
    ln_g = din("ln_g", [4, D])
    ln_b = din("ln_b", [4, D])
    w1i = din("ffn1_w_in", [D, 2 * DFF])
    w1o = din("ffn1_w_out", [DFF, D])
    w_in = din("w_in", [D, 2320])
    q_norm = din("q_norm", [64])
    k_norm = din("k_norm", [64])
    conv_w = din("conv_w", [5, 1024])
    conv_b = din("conv_b", [1024])
    dt_bias = din("dt_bias", [16])
    a_log = din("a_log", [16])
    d_skip = din("d_skip", [8])
    ssd_norm = din("ssd_norm", [512])
    w_out = din("w_out", [D, D])
    w2i = din("ffn2_w_in", [D, 2 * DFF])
    w2o = din("ffn2_w_out", [DFF, D])
    ple_w = din("ple_w", [256, D])
    ple_gw = din("ple_gate_w", [D, D])
    ple_gb = din("ple_gate_b", [D])
    c_ident = din("c_ident", [128, 128])
    c_masks = din("c_masks", [4, 128, 128])
    c_cos = din("c_cos", [SEQ, 32])
    c_sin = din("c_sin", [SEQ, 32])
    out_d = nc.dram_tensor("out", [nseq, SEQ, D], F32, kind="ExternalOutput").ap()
    dbg_d = None
    if dbg is not None:
        dbg_d = nc.dram_tensor("dbg", [SEQ, D], F32, kind="ExternalOutput").ap()

    with ExitStack() as stack:
        S = Sched(nc, stack)

        uid = [0]

        def sbt(st, name, shape, dt):
            uid[0] += 1
            return st.enter_context(nc.sbuf_tensor(f"{name}_{uid[0]}", list(shape), dt))

        def MM(out, lhsT, rhs, start, stop, reads, writes):
            S.op("pe", "matmul", dict(out=out, lhsT=lhsT, rhs=rhs, start=start, stop=stop), reads, writes)

        def TR(out, in_, ident, reads, writes):
            S.op("pe", "transpose", dict(out=out, in_=in_, identity=ident), list(reads) + [r_const], writes)

        def ACT(out, in_, func, reads, writes, bias=None, scale=None, accum_out=None):
            kw = dict(out=out, in_=in_, func=func)
            if bias is not None:
                kw["bias"] = bias
            if scale is not None:
                kw["scale"] = scale
            if accum_out is not None:
                kw["accum_out"] = accum_out
            S.op("act", "activation", kw, reads, writes)

        def ACTcopy(out, in_, reads, writes):
            S.op("act", "copy", dict(out=out, in_=in_), reads, writes)

        def TT(eng, out, in0, in1, op, reads, writes):
            S.op(eng, "tensor_tensor", dict(out=out, in0=in0, in1=in1, op=op), reads, writes)

        def TS(eng, out, in0, s1, s2, op0, op1, reads, writes):
            kw = dict(out=out, in0=in0, scalar1=s1, scalar2=s2, op0=op0)
            if op1 is not None:
                kw["op1"] = op1
            S.op(eng, "tensor_scalar", kw, reads, writes)

        def STT(eng, out, in0, scalar, in1, op0, op1, reads, writes):
            S.op(eng, "scalar_tensor_tensor", dict(out=out, in0=in0, scalar=scalar, in1=in1, op0=op0, op1=op1), reads, writes)

        def CP(eng, out, in_, reads, writes):
            S.op(eng, "tensor_copy", dict(out=out, in_=in_), reads, writes)

        def MEMSET(eng, ap, val, writes):
            S.op(eng, "memset", dict(ap=ap, constant=val), (), writes)

        def DMA(eng, out, in_, key, reads, writes, slow=False):
            kw = dict(out=out, in_=in_)
            if slow:
                kw["allow_slow_non_contiguous"] = True
            S.op(eng, "dma_start", kw, reads, writes, dma_key=key)

        acc = sbt(stack, "acc", [128, NT, D], F32)
        r_acc = [S.res(f"acc{t}") for t in range(NT)]
        xT = sbt(stack, "xT", [128, 8, SEQ], BF16)
        r_xT = [S.res(f"xT{t}") for t in range(NT)]
        identf = sbt(stack, "identf", [128, 128], F32)
        identb = sbt(stack, "identb", [128, 128], BF16)
        masks = sbt(stack, "masks", [128, 4, 128], F32)
        onesf = sbt(stack, "onesf", [128, 128], F32)
        cwall = sbt(stack, "cwall", [128, 8, 5], F32)
        cball = sbt(stack, "cball", [128, 8], F32)
        dtb_bc = sbt(stack, "dtb_bc", [128, 16], F32)
        A_bc = sbt(stack, "A_bc", [128, 16], F32)
        dsk_bc = sbt(stack, "dsk_bc", [128, 8], F32)
        nw = sbt(stack, "nw", [128, 2, 64], F32)
        snw = sbt(stack, "snw", [128, 512], F32)
        r_const = S.res("const")
        banks = [stack.enter_context(nc.psum_tensor(f"bank{i}", [128, 512], F32)) for i in range(8)]
        r_bank = [S.res(f"bank{i}", excl=True) for i in range(8)]
        M_LE, M_GE, M_GT, M_LT = (masks[:, i, :] for i in range(4))

        DMA("sp", identf[:], c_ident, "c0", [], [r_const])
        DMA("sp", masks[:], c_masks.rearrange("m p j -> p m j"), "c1", [], [r_const])
        for jj in range(5):
            DMA("sp", cwall[:, :, jj], conv_w[jj].rearrange("(cc p) -> p cc", p=128), "c2", [], [r_const], slow=True)
        DMA("sp", cball[:], conv_b.rearrange("(cc p) -> p cc", p=128), "c3", [], [r_const], slow=True)
        DMA("sp", dtb_bc[:], dt_bias.partition_broadcast(128), "c4", [], [r_const])
        DMA("sp", A_bc[:], a_log.partition_broadcast(128), "c5", [], [r_const])
        DMA("sp", dsk_bc[:], d_skip.partition_broadcast(128), "c6", [], [r_const])
        DMA("sp", nw[:, 0, :], q_norm.partition_broadcast(128), "c7", [], [r_const])
        DMA("sp", nw[:, 1, :], k_norm.partition_broadcast(128), "c8", [], [r_const])
        DMA("sp", snw[:], ssd_norm.partition_broadcast(128), "c9", [], [r_const])
        CP("dve", identb[:], identf[:], [r_const], [r_const])
        MEMSET("dve", onesf[:], 1.0, [r_const])
        ACT(A_bc[:], A_bc[:], AF.Exp, [r_const], [r_const])
        TS("dve", A_bc[:], A_bc[:], -1.0, None, ALU.mult, None, [r_const], [r_const])

        class Rot:
            def __init__(self, st, name, shape, dt, n):
                self.tiles = [sbt(st, f"{name}{i}", shape, dt) for i in range(n)]
                self.res = [S.res(f"{name}{i}") for i in range(n)]
                self.keys = [f"{name}{i}" for i in range(n)]
                self.n = n
                self.i = -1

            def next(self):
                self.i = (self.i + 1) % self.n
                return self.tiles[self.i], self.res[self.i], self.keys[self.i]

        class BankRot:
            def __init__(self, ids):
                self.ids = ids
                self.i = -1

            def next(self):
                self.i = (self.i + 1) % len(self.ids)
                b = self.ids[self.i]
                return banks[b], r_bank[b]

        def bf16view(bk, a):
            return bk[:].bitcast(BF16).rearrange("p (a b) -> p a b", a=a)

        def prep(t, xbrot, bankrot):
            xb, r_xb, _ = xbrot.next()
            CP("dve", xb[:], acc[:, t, :], [r_acc[t]], [r_xb])
            S.op("act", "mul", dict(out=acc[:, t, :], in_=acc[:, t, :], mul=ALPHA), [r_acc[t]], [r_acc[t]])
            bk, r_bk = bankrot.next()
            bkb = bf16view(bk, 8)
            for kc in range(8):
                TR(bkb[:, kc, :], xb[:, kc * 128:(kc + 1) * 128], identb[:], [r_xb], [r_bk])
            ACTcopy(xT[:, :, t * 128:(t + 1) * 128], bkb, [r_bk], [r_xT[t]])

        def layernorm(t, lnrot, g_bc, b_bc, r_ln):
            st6, r_st, _ = lnrot.next()
            bst = st6[:, 0:12].rearrange("p (a b) -> p a b", a=2)
            mv = st6[:, 12:14]
            tmp = st6[:, 14:16]
            for hf in range(2):
                S.op("dve", "bn_stats", dict(out=bst[:, hf, :], in_=acc[:, t, hf * 512:(hf + 1) * 512]), [r_acc[t]], [r_st])
            S.op("dve", "bn_aggr", dict(out=mv, in_=bst), [r_st], [r_st])
            ACT(tmp[:, 0:1], mv[:, 1:2], AF.Ln, [r_st], [r_st], bias=LN_EPS)
            ACT(tmp[:, 0:1], tmp[:, 0:1], AF.Exp, [r_st], [r_st], scale=-0.5)
            TS("dve", tmp[:, 1:2], mv[:, 0:1], tmp[:, 0:1], -1.0, ALU.mult, ALU.mult, [r_st], [r_st])
            ACT(acc[:, t, :], acc[:, t, :], AF.Identity, [r_st, r_acc[t]], [r_acc[t]], bias=tmp[:, 1:2], scale=tmp[:, 0:1])
            TT("pool", acc[:, t, :], acc[:, t, :], g_bc[:], ALU.mult, [r_acc[t], r_ln], [r_acc[t]])
            TT("pool", acc[:, t, :], acc[:, t, :], b_bc[:], ALU.add, [r_acc[t], r_ln], [r_acc[t]])

        def load_ln(idx, g_bc, b_bc, r_ln):
            DMA("sp", g_bc[:], ln_g[idx].partition_broadcast(128), "lng", [], [r_ln])
            DMA("sp", b_bc[:], ln_b[idx].partition_broadcast(128), "lnb", [], [r_ln])

        def ffn_phase(wi_d, wo_d, ln_idx, load_x_seq=None):
            with ExitStack() as st:
                xbrot = Rot(st, "xb", [128, D], BF16, 2)
                wg = Rot(st, "wg", [128, 8, FG * 128], BF16, 2)
                wu = Rot(st, "wu", [128, 8, FG * 128], BF16, 2)
                wo = Rot(st, "wo", [128, D], BF16, 6)
                hrot = Rot(st, "h", [128, SEQ], BF16, 4)
                sgrot = Rot(st, "sg", [128, 512], BF16, 3)
                lnrot = Rot(st, "lnst", [128, 16], F32, 4)
                g_bc = sbt(st, "g_bc", [128, D], F32)
                b_bc = sbt(st, "b_bc", [128, D], F32)
                r_ln = S.res("lnp")
                load_ln(ln_idx, g_bc, b_bc, r_ln)
                prep_banks = BankRot([4, 5, 6, 7])
                for t in range(NT):
                    if load_x_seq is not None:
                        DMA("sp", acc[:, t, :], x_d[load_x_seq, t * 128:(t + 1) * 128, :], f"xin{t}", [], [r_acc[t]])
                    prep(t, xbrot, prep_banks)
                gate_banks = BankRot([0, 1])
                up_banks = BankRot([2, 3])
                y_banks = BankRot([4, 5, 6, 7])
                ngroups = NF // FG

                def load_group(gi):
                    f0 = gi * FG
                    wgt, r_wg, k_wg = wg.next()
                    wut, r_wu, k_wu = wu.next()
                    DMA("pool", wgt[:], wi_d[:, f0 * 128:(f0 + FG) * 128].rearrange("(kc p) f -> p kc f", p=128), k_wg, [], [r_wg])
                    DMA("pool", wut[:], wi_d[:, DFF + f0 * 128:DFF + (f0 + FG) * 128].rearrange("(kc p) f -> p kc f", p=128), k_wu, [], [r_wu])
                    wos = []
                    for j in range(FG):
                        wot, r_wo, k_wo = wo.next()
                        DMA("pool", wot[:], wo_d[(f0 + j) * 128:(f0 + j + 1) * 128, :], k_wo, [], [r_wo])
                        wos.append((wot, r_wo))
                    return (wgt, r_wg, wut, r_wu, wos)

                def first(W):
                    wgt, r_wg, wut, r_wu, wos = W
                    hs = []
                    for j in range(FG):
                        ht, r_h, _ = hrot.next()
                        hs.append((ht, r_h))
                        for tb in range(4):
                            gb, r_gb = gate_banks.next()
                            ub, r_ub = up_banks.next()
                            rx = [r_xT[tb * 4 + i] for i in range(4)]
                            for kc in range(8):
                                MM(gb[:], wgt[:, kc, j * 128:(j + 1) * 128], xT[:, kc, tb * 512:(tb + 1) * 512], kc == 0, kc == 7, [r_wg] + rx, [r_gb])
                            for kc in range(8):
                                MM(ub[:], wut[:, kc, j * 128:(j + 1) * 128], xT[:, kc, tb * 512:(tb + 1) * 512], kc == 0, kc == 7, [r_wu] + rx, [r_ub])
                            sg, r_sg, _ = sgrot.next()
                            ACT(sg[:], gb[:], AF.Silu, [r_gb], [r_sg])
                            TT("dve", ht[:, tb * 512:(tb + 1) * 512], ub[:], sg[:], ALU.mult, [r_ub, r_sg], [r_h])
                    return hs

                def second(W, hs, last):
                    wgt, r_wg, wut, r_wu, wos = W
                    for t in range(NT):
                        yb = [y_banks.next() for _ in range(2)]
                        for j in range(FG):
                            ht, r_h = hs[j]
                            wot, r_wo = wos[j]
                            for hf in range(2):
                                MM(yb[hf][0][:], ht[:, t * 128:(t + 1) * 128], wot[:, hf * 512:(hf + 1) * 512], j == 0, j == FG - 1,
                                   [r_h, r_wo], [yb[hf][1]])
                        for hf in range(2):
                            STT("dve", acc[:, t, hf * 512:(hf + 1) * 512], yb[hf][0][:], 0.5, acc[:, t, hf * 512:(hf + 1) * 512],
                                ALU.mult, ALU.add, [yb[hf][1], r_acc[t]], [r_acc[t]])
                        if last:
                            layernorm(t, lnrot, g_bc, b_bc, r_ln)

                prevW = None
                prevH = None
                for gi in range(ngroups):
                    W = load_group(gi)
                    hs = first(W)
                    if prevW is not None:
                        second(prevW, prevH, False)
                    prevW, prevH = W, hs
                second(prevW, prevH, True)
                S.barrier()

        def attn_phase():
            with ExitStack() as st:
                xbrot = Rot(st, "xb", [128, D], BF16, 2)
                wq = sbt(st, "wq", [128, 8, 768], BF16)
                r_wq = S.res("wq")
                cs = sbt(st, "cs", [128, NT, 32], F32)
                sn = sbt(st, "sn", [128, NT, 32], F32)
                r_cs = S.res("cs")
                qT = sbt(st, "qT", [128, 4, SEQ], BF16)
                r_qT = [S.res(f"qT{t}") for t in range(NT)]
                kT = sbt(st, "kT", [128, SEQ], BF16)
                r_kT = [S.res(f"kT{t}") for t in range(NT)]
                va = sbt(st, "va", [128, NT, 2, 128], BF16)
                r_va = [S.res(f"va{t}") for t in range(NT)]
                woa = sbt(st, "woa", [128, 4, D], BF16)
                r_woa = S.res("woa")
                attT = Rot(st, "attT", [128, 4, 512], BF16, 2)
                pt = Rot(st, "pt", [128, 512], BF16, 4)
                rec = Rot(st, "rec", [128, 512], F32, 2)
                qfr = Rot(st, "qf", [128, 640], F32, 2)
                sqr = Rot(st, "sq", [128, 640], F32, 1)
                ssr = Rot(st, "ssr", [128, 32], F32, 2)
                rar = Rot(st, "ra", [128, 320], F32, 2)
                rbr = Rot(st, "rb", [128, 320], F32, 2)
                qrr = Rot(st, "qr", [128, 4, 2, 64], BF16, 2)
                krr = Rot(st, "kr", [128, 128], BF16, 2)

                DMA("pool", wq[:], w_in[:, 0:768].rearrange("(kc p) f -> p kc f", p=128), "wq", [], [r_wq])
                DMA("sp", cs[:], c_cos.rearrange("(t p) f -> p t f", p=128), "cs", [], [r_cs])
                DMA("sp", sn[:], c_sin.rearrange("(t p) f -> p t f", p=128), "sn", [], [r_cs])
                for g in range(2):
                    for j in range(4):
                        hh = g * 4 + j
                        DMA("pool", woa[g * 64:(g + 1) * 64, j, :], w_out[hh * 64:(hh + 1) * 64, :], f"woa{(g * 4 + j) % 2}", [], [r_woa])
                MEMSET("pool", va[:, :, 0, 64:128], 1.0, r_va)
                MEMSET("pool", va[:, :, 1, 0:64], 1.0, r_va)

                prep_banks = BankRot([6, 7])
                q_banks = BankRot([0, 1])
                kv_banks = BankRot([2, 3])
                tr_banks = BankRot([4, 5])
                tasks = []
                for t in range(NT):
                    S.begin_task()
                    prep(t, xbrot, prep_banks)
                    qb_, r_qb = q_banks.next()
                    kvb, r_kvb = kv_banks.next()
                    for kc in range(8):
                        MM(qb_[:], xT[:, kc, t * 128:(t + 1) * 128], wq[:, kc, 0:512], kc == 0, kc == 7, [r_xT[t], r_wq], [r_qb])
                    for kc in range(8):
                        MM(kvb[:, 0:256], xT[:, kc, t * 128:(t + 1) * 128], wq[:, kc, 512:768], kc == 0, kc == 7, [r_xT[t], r_wq], [r_kvb])
                    S.next_stage()
                    qf, r_qf, _ = qfr.next()
                    ACTcopy(qf[:, 0:512], qb_[:], [r_qb], [r_qf])
                    ACTcopy(qf[:, 512:640], kvb[:, 0:128], [r_kvb], [r_qf])
                    ACTcopy(va[:, t, 0, 0:64], kvb[:, 128:192], [r_kvb], [r_va[t]])
                    ACTcopy(va[:, t, 1, 64:128], kvb[:, 192:256], [r_kvb], [r_va[t]])
                    sq, r_sq, _ = sqr.next()
                    ss, r_ss, _ = ssr.next()
                    qf3 = qf[:].rearrange("p (h d) -> p h d", d=64)
                    TT("dve", sq[:], qf[:], qf[:], ALU.mult, [r_qf], [r_sq])
                    S.op("dve", "tensor_reduce", dict(out=ss[:, 0:10], in_=sq[:].rearrange("p (h d) -> p h d", d=64), axis=AX.X, op=ALU.add),
                         [r_sq], [r_ss])
                    ACT(ss[:, 10:20], ss[:, 0:10], AF.Ln, [r_ss], [r_ss], bias=RMS_EPS, scale=1.0 / 64)
                    ACT(ss[:, 20:30], ss[:, 10:20], AF.Exp, [r_ss], [r_ss], scale=-0.5)
                    TT("dve", qf3, qf3, ss[:, 20:30].unsqueeze(2).to_broadcast([128, 10, 64]), ALU.mult, [r_qf, r_ss], [r_qf])
                    TT("dve", qf3[:, 0:8, :], qf3[:, 0:8, :], nw[:, 0:1, :].to_broadcast([128, 8, 64]), ALU.mult, [r_qf, r_const], [r_qf])
                    TT("dve", qf3[:, 8:10, :], qf3[:, 8:10, :], nw[:, 1:2, :].to_broadcast([128, 2, 64]), ALU.mult, [r_qf, r_const], [r_qf])
                    x0 = qf3[:, :, 0::2]
                    x1 = qf3[:, :, 1::2]
                    cb = cs[:, t, :].unsqueeze(1).to_broadcast([128, 10, 32])
                    sb_ = sn[:, t, :].unsqueeze(1).to_broadcast([128, 10, 32])
                    ra, r_ra, _ = rar.next()
                    rb, r_rb, _ = rbr.next()
                    ra3 = ra[:].rearrange("p (h d) -> p h d", d=32)
                    rb3 = rb[:].rearrange("p (h d) -> p h d", d=32)
                    qr, r_qr, _ = qrr.next()
                    kr, r_kr, _ = krr.next()
                    qrv = qr[:].rearrange("p j g d -> p g j d")
                    kr3 = kr[:].rearrange("p (k d) -> p k d", d=64)
                    TT("dve", ra3, x0, cb, ALU.mult, [r_qf, r_cs], [r_ra])
                    TT("dve", rb3, x1, sb_, ALU.mult, [r_qf, r_cs], [r_rb])
                    TT("dve", qrv[:, :, :, 0::2], ra3[:, 0:8, :].rearrange("p (g j) d -> p g j d", g=2),
                       rb3[:, 0:8, :].rearrange("p (g j) d -> p g j d", g=2), ALU.subtract, [r_ra, r_rb], [r_qr])
                    TT("dve", kr3[:, :, 0::2], ra3[:, 8:10, :], rb3[:, 8:10, :], ALU.subtract, [r_ra, r_rb], [r_kr])
                    TT("dve", ra3, x0, sb_, ALU.mult, [r_qf, r_cs], [r_ra])
                    TT("dve", rb3, x1, cb, ALU.mult, [r_qf, r_cs], [r_rb])
                    TT("dve", qrv[:, :, :, 1::2], ra3[:, 0:8, :].rearrange("p (g j) d -> p g j d", g=2),
                       rb3[:, 0:8, :].rearrange("p (g j) d -> p g j d", g=2), ALU.add, [r_ra, r_rb], [r_qr])
                    TT("dve", kr3[:, :, 1::2], ra3[:, 8:10, :], rb3[:, 8:10, :], ALU.add, [r_ra, r_rb], [r_kr])
                    S.next_stage()
                    trb, r_trb = tr_banks.next()
                    trv = bf16view(trb, 8)
                    for j in range(4):
                        TR(trv[:, j, :], qr[:, j, :, :].rearrange("p g d -> p (g d)"), identb[:], [r_qr], [r_trb])
                    TR(trv[:, 4, :], kr[:], identb[:], [r_kr], [r_trb])
                    ACTcopy(qT[:, :, t * 128:(t + 1) * 128], trv[:, 0:4, :], [r_trb], [r_qT[t]])
                    ACTcopy(kT[:, t * 128:(t + 1) * 128], trv[:, 4, :], [r_trb], [r_kT[t]])
                    tasks.append(S.end_task())
                S.replay_skewed(tasks)

                sc_banks = BankRot([0, 1, 2, 3])
                ov_banks = BankRot([4, 5])
                o_banks = BankRot([6, 7])
                LOOK = 2
                for qb in range(4):
                    at, r_at, _ = attT.next()
                    rq = [r_qT[qb * 4 + i] for i in range(4)]
                    its = [(g, j, kc) for g in range(2) for j in range(4) for kc in range(NT)]
                    pend = []
                    ovs = {}
                    for i in range(len(its) + LOOK):
                        if i < len(its):
                            g, j, kc = its[i]
                            vr = slice(g * 64, (g + 1) * 64)
                            sc, r_sc = sc_banks.next()
                            MM(sc[:], kT[vr, kc * 128:(kc + 1) * 128], qT[vr, j, qb * 512:(qb + 1) * 512], True, True,
                               [r_kT[kc]] + rq, [r_sc])
                            ptt, r_pt, _ = pt.next()
                            ACT(ptt[:], sc[:], AF.Exp, [r_sc], [r_pt], scale=0.125)
                            pend.append((g, j, kc, ptt, r_pt))
                        if i >= LOOK:
                            g, j, kc, ptt, r_pt = pend[i - LOOK]
                            vr = slice(g * 64, (g + 1) * 64)
                            sr = slice((1 - g) * 64, (2 - g) * 64)
                            if kc == 0:
                                ovs[(g, j)] = ov_banks.next()
                            ov, r_ov = ovs[(g, j)]
                            MM(ov[:], va[:, kc, g, :], ptt[:], kc == 0, kc == NT - 1, [r_va[kc], r_pt], [r_ov])
                            if kc == NT - 1:
                                rc, r_rc, _ = rec.next()
                                S.op("dve", "reciprocal", dict(out=rc[vr, :], in_=ov[sr, :]), [r_ov], [r_rc])
                                TT("dve", at[vr, j, :], ov[vr, :], rc[vr, :], ALU.mult, [r_ov, r_rc], [r_at])
                    for tt in range(4):
                        t = qb * 4 + tt
                        yb = [o_banks.next() for _ in range(2)]
                        for hf in range(2):
                            for j in range(4):
                                MM(yb[hf][0][:], at[:, j, tt * 128:(tt + 1) * 128], woa[:, j, hf * 512:(hf + 1) * 512], j == 0, j == 3,
                                   [r_at, r_woa], [yb[hf][1]])
                        for hf in range(2):
                            TT("dve", acc[:, t, hf * 512:(hf + 1) * 512], yb[hf][0][:], acc[:, t, hf * 512:(hf + 1) * 512], ALU.add,
                               [yb[hf][1], r_acc[t]], [r_acc[t]])
                S.barrier()

        def ssd_phase():
            with ExitStack() as st:
                allb = BankRot([0, 1, 2, 3, 4, 5, 6, 7])
                wdt = sbt(st, "wdt", [128, 8, 16], BF16)
                r_wdt = S.res("wdt")
                dtall = sbt(st, "dtall", [128, NT, 16], F32)
                aall = sbt(st, "aall", [128, NT, 16], F32)
                Eall = sbt(st, "Eall", [128, NT, 48], F32)
                sfd = sbt(st, "sfd", [128, NT, 8], F32)
                sbd = sbt(st, "sbd", [128, NT, 8], F32)
                r_dt = S.res("dt")
                wg4 = sbt(st, "wg4", [128, 8, 512], BF16)
                r_wg4 = S.res("wg4")
                wz = sbt(st, "wz", [128, 8, 256], BF16)
                r_wz = S.res("wz")
                wos = sbt(st, "wos", [128, 2, D], BF16)
                r_wos = S.res("wos")
                pre = sbt(st, "pre", [128, SEQ + 4], F32)
                r_pre = S.res("pre")
                xsT = sbt(st, "xsT", [128, 2, SEQ], F32)
                r_xsT = [S.res("xsT0"), S.res("xsT1")]
                BT = sbt(st, "BT", [128, SEQ], BF16)
                CT = sbt(st, "CT", [128, SEQ], BF16)
                r_BT = S.res("BT")
                r_CT = S.res("CT")
                hbst = sbt(st, "hbst", [128, NT, 256], BF16)
                r_hbst = [S.res(f"hbst{c}") for c in range(NT)]
                Hf = sbt(st, "Hf", [128, 256], F32)
                Hb = sbt(st, "Hb", [128, 256], F32)
                r_Hf = S.res("Hf")
                r_Hb = S.res("Hb")
                xdf_r = Rot(st, "xdf", [128, 256], BF16, 2)
                xdb_r = Rot(st, "xdb", [128, 256], BF16, 2)
                xdd_r = Rot(st, "xdd", [128, 256], BF16, 2)
                xD_r = Rot(st, "xD", [128, 256], F32, 2)
                btm_r = Rot(st, "btm", [128, 128], BF16, 2)
                cbf_r = Rot(st, "cbf", [128, 128], F32, 2)
                cbb_r = Rot(st, "cbb", [128, 128], F32, 2)
                R_r = Rot(st, "R", [128, 512], F32, 1)
                E_r = Rot(st, "E", [128, 512], F32, 1)
                Mf_r = Rot(st, "Mf", [128, 512], BF16, 2)
                Mb_r = Rot(st, "Mb", [128, 512], BF16, 2)
                y_r = Rot(st, "y", [128, 256], F32, 1)
                t1_r = Rot(st, "t1", [128, 256], F32, 1)
                sz_r = Rot(st, "sz", [128, 256], F32, 1)
                yg_r = Rot(st, "yg", [128, 256], F32, 1)
                junk_r = Rot(st, "junk", [128, 256], F32, 1)
                yn_r = Rot(st, "yn", [128, 256], BF16, 2)
                ynT_r = Rot(st, "ynT", [128, 2, 128], BF16, 2)
                hfb_r = Rot(st, "hfb", [128, 256], BF16, 2)
                ssq_r = Rot(st, "ssq", [128, 4], F32, 2)

                DMA("pool", wdt[:], w_in[:, C_DT:C_DT + 16].rearrange("(kc p) f -> p kc f", p=128), "wdt", [], [r_wdt])
                bk, r_bk = allb.next()
                for t in range(NT):
                    for kc in range(8):
                        MM(bk[:, t * 16:(t + 1) * 16], xT[:, kc, t * 128:(t + 1) * 128], wdt[:, kc, :], kc == 0, kc == 7, [r_xT[t], r_wdt], [r_bk])
                TT("dve", dtall[:], bk[:, 0:256].rearrange("p (t f) -> p t f", f=16), dtb_bc[:].unsqueeze(1).to_broadcast([128, NT, 16]),
                   ALU.add, [r_bk, r_const], [r_dt])
                ACT(dtall[:], dtall[:], AF.Exp, [r_dt], [r_dt])
                ACT(dtall[:], dtall[:], AF.Ln, [r_dt], [r_dt], bias=1.0)
                TT("dve", aall[:], dtall[:], A_bc[:].unsqueeze(1).to_broadcast([128, NT, 16]), ALU.mult, [r_dt, r_const], [r_dt])
                for half in range(2):
                    bk, r_bk = allb.next()
                    for c8 in range(8):
                        c = half * 8 + c8
                        o0 = c8 * 48
                        MM(bk[:, o0:o0 + 8], M_LE, aall[:, c, 0:8], True, True, [r_dt, r_const], [r_bk])
                        MM(bk[:, o0 + 8:o0 + 16], M_GT, aall[:, c, 0:8], True, True, [r_dt, r_const], [r_bk])
                        MM(bk[:, o0 + 16:o0 + 24], M_LT, aall[:, c, 8:16], True, True, [r_dt, r_const], [r_bk])
                        MM(bk[:, o0 + 24:o0 + 32], M_GE, aall[:, c, 8:16], True, True, [r_dt, r_const], [r_bk])
                        MM(bk[:, o0 + 32:o0 + 48], onesf[:], aall[:, c, :], True, True, [r_dt, r_const], [r_bk])
                    ACT(Eall[:, half * 8:(half + 1) * 8, :], bk[:, 0:384].rearrange("p (c f) -> p c f", f=48), AF.Exp, [r_bk], [r_dt])
                TT("dve", sfd[:], dtall[:, :, 0:8], Eall[:, :, 8:16], ALU.mult, [r_dt], [r_dt])
                TT("dve", sbd[:], dtall[:, :, 8:16], Eall[:, :, 16:24], ALU.mult, [r_dt], [r_dt])

                MEMSET("pool", pre[:, 0:2], 0.0, [r_pre])
                MEMSET("pool", pre[:, SEQ + 2:SEQ + 4], 0.0, [r_pre])
                if SSD_STOP[0] == 1:
                    S.barrier()
                    return

                def bc4(ap4):
                    return ap4.unsqueeze(2).to_broadcast([128, 4, 64])

                def v4(ap):
                    return ap.rearrange("p (r d) -> p r d", d=64)

                for g in range(2):
                    DMA("pool", wg4[:, :, 0:256], w_in[:, C_XS + g * 256:C_XS + (g + 1) * 256].rearrange("(kc p) f -> p kc f", p=128), "wg4a", [], [r_wg4])
                    DMA("pool", wg4[:, :, 256:384], w_in[:, C_B + g * 128:C_B + (g + 1) * 128].rearrange("(kc p) f -> p kc f", p=128), "wg4b", [], [r_wg4])
                    DMA("pool", wg4[:, :, 384:512], w_in[:, C_C + g * 128:C_C + (g + 1) * 128].rearrange("(kc p) f -> p kc f", p=128), "wg4c", [], [r_wg4])
                    DMA("pool", wz[:], w_in[:, C_Z + g * 256:C_Z + (g + 1) * 256].rearrange("(kc p) f -> p kc f", p=128), "wz", [], [r_wz])
                    for j in range(2):
                        r0 = 512 + g * 256 + j * 128
                        DMA("pool", wos[:, j, :], w_out[r0:r0 + 128, :], f"wos{j}", [], [r_wos])
                    plan = [(2, 4 + g, 0, "B"), (3, 6 + g, 1, "C"), (0, g * 2, 0, "x"), (1, g * 2 + 1, 1, "x")]
                    for (cc, ch, slot, kind) in plan:
                        for tb in range(4):
                            bk, r_bk = allb.next()
                            rx = [r_xT[tb * 4 + i] for i in range(4)]
                            for kc in range(8):
                                MM(bk[:], wg4[:, kc, cc * 128:(cc + 1) * 128], xT[:, kc, tb * 512:(tb + 1) * 512], kc == 0, kc == 7, [r_wg4] + rx, [r_bk])
                            ACTcopy(pre[:, 2 + tb * 512:2 + (tb + 1) * 512], bk[:], [r_bk], [r_pre])
                        eng = "dve"
                        o = xsT[:, slot, :]
                        r_o = r_xsT[slot]
                        TS(eng, o, pre[:, 0:SEQ], cwall[:, ch, 0:1], None, ALU.mult, None, [r_pre, r_const], [r_o])
                        for jj in range(1, 5):
                            STT(eng, o, pre[:, jj:jj + SEQ], cwall[:, ch, jj:jj + 1], o, ALU.mult, ALU.add, [r_pre, r_const, r_o], [r_o])
                        if kind == "B":
                            ACT(BT[:], o, AF.Silu, [r_o, r_const], [r_BT], bias=cball[:, ch:ch + 1])
                        elif kind == "C":
                            ACT(CT[:], o, AF.Silu, [r_o, r_const], [r_CT], bias=cball[:, ch:ch + 1])
                        else:
                            ACT(o, o, AF.Silu, [r_o, r_const], [r_o], bias=cball[:, ch:ch + 1])

                    if SSD_STOP[0] == 2:
                        S.barrier()
                        return

                    def chunk_common(c, need_b):
                        bx, r_bx = allb.next()
                        for cc in range(2):
                            TR(bx[:, cc * 128:(cc + 1) * 128], xsT[:, cc, c * 128:(c + 1) * 128], identf[:], [r_xsT[cc]], [r_bx])
                        btm, r_btm = None, None
                        if need_b:
                            bb, r_bb = allb.next()
                            bbv = bf16view(bb, 8)
                            TR(bbv[:, 0, :], BT[:, c * 128:(c + 1) * 128], identb[:], [r_BT], [r_bb])
                            btm, r_btm, _ = btm_r.next()
                            ACTcopy(btm[:], bbv[:, 0, :], [r_bb], [r_btm])
                        return bx, r_bx, btm, r_btm

                    MEMSET("pool", Hb[:], 0.0, [r_Hb])
                    for c in range(NT - 1, -1, -1):
                        ACTcopy(hbst[:, c, :], Hb[:], [r_Hb], [r_hbst[c]])
                        if c == 0:
                            break
                        bx, r_bx, btm, r_btm = chunk_common(c, True)
                        xdd, r_xdd, _ = xdd_r.next()
                        TT("dve", v4(xdd[:]), v4(bx[:, 0:256]), bc4(sbd[:, c, g * 4:g * 4 + 4]), ALU.mult, [r_bx, r_dt], [r_xdd])
                        bs, r_bs = allb.next()
                        MM(bs[:, 0:256], btm[:], xdd[:], True, True, [r_btm, r_xdd], [r_bs])
                        TT("dve", v4(Hb[:]), v4(Hb[:]), bc4(Eall[:, c, 40 + g * 4:44 + g * 4]), ALU.mult, [r_Hb, r_dt], [r_Hb])
                        TT("dve", Hb[:], bs[:, 0:256], Hb[:], ALU.add, [r_bs, r_Hb], [r_Hb])

                    if SSD_STOP[0] == 3:
                        S.barrier()
                        return
                    MEMSET("pool", Hf[:], 0.0, [r_Hf])
                    for c in range(NT):
                        bx, r_bx, btm, r_btm = chunk_common(c, True)
                        xdf, r_xdf, _ = xdf_r.next()
                        xdb, r_xdb, _ = xdb_r.next()
                        xdd, r_xdd, _ = xdd_r.next()
                        xD, r_xD, _ = xD_r.next()
                        xt4 = v4(bx[:, 0:256])
                        TT("dve", v4(xdf[:]), xt4, bc4(dtall[:, c, g * 4:g * 4 + 4]), ALU.mult, [r_bx, r_dt], [r_xdf])
                        TT("dve", v4(xdb[:]), xt4, bc4(dtall[:, c, 8 + g * 4:12 + g * 4]), ALU.mult, [r_bx, r_dt], [r_xdb])
                        TT("dve", v4(xdd[:]), xt4, bc4(sfd[:, c, g * 4:g * 4 + 4]), ALU.mult, [r_bx, r_dt], [r_xdd])
                        TT("dve", v4(xD[:]), xt4, bc4(dsk_bc[:, g * 4:g * 4 + 4]), ALU.mult, [r_bx, r_const], [r_xD])
                        bc_, r_bc = allb.next()
                        MM(bc_[:, 0:128], BT[:, c * 128:(c + 1) * 128], CT[:, c * 128:(c + 1) * 128], True, True, [r_BT, r_CT], [r_bc])
                        cbf, r_cbf, _ = cbf_r.next()
                        cbb, r_cbb, _ = cbb_r.next()
                        TT("dve", cbf[:], bc_[:, 0:128], M_LE, ALU.mult, [r_bc, r_const], [r_cbf])
                        TT("dve", cbb[:], bc_[:, 0:128], M_GE, ALU.mult, [r_bc, r_const], [r_cbb])
                        Ms = []
                        for (dr, mask_r, mask_l, cbx, r_cbx, Mrot) in ((0, M_LE, M_GT, cbf, r_cbf, Mf_r), (1, M_GE, M_LT, cbb, r_cbb, Mb_r)):
                            Rt, r_R, _ = R_r.next()
                            R3 = Rt[:].rearrange("p (r l) -> p r l", l=128)
                            a4 = aall[:, c, dr * 8 + g * 4:dr * 8 + g * 4 + 4]
                            TT("dve", R3, mask_r.unsqueeze(1).to_broadcast([128, 4, 128]), a4.unsqueeze(2).to_broadcast([128, 4, 128]), ALU.mult,
                               [r_const, r_dt], [r_R])
                            ba, r_ba = allb.next()
                            MM(ba[:], mask_l, Rt[:], True, True, [r_const, r_R], [r_ba])
                            Et, r_E, _ = E_r.next()
                            ACT(Et[:], ba[:], AF.Exp, [r_ba], [r_E])
                            Mt, r_M, _ = Mrot.next()
                            TT("dve", Mt[:].rearrange("p (r l) -> p r l", l=128), Et[:].rearrange("p (r l) -> p r l", l=128),
                               cbx[:].unsqueeze(1).to_broadcast([128, 4, 128]), ALU.mult, [r_E, r_cbx], [r_M])
                            Ms.append((Mt, r_M))
                        yd, r_yd = allb.next()
                        for r in range(4):
                            MM(yd[:, r * 64:(r + 1) * 64], Ms[0][0][:, r * 128:(r + 1) * 128], xdf[:, r * 64:(r + 1) * 64], True, False,
                               [Ms[0][1], r_xdf], [r_yd])
                            MM(yd[:, r * 64:(r + 1) * 64], Ms[1][0][:, r * 128:(r + 1) * 128], xdb[:, r * 64:(r + 1) * 64], False, True,
                               [Ms[1][1], r_xdb], [r_yd])
                        hfb, r_hfb, _ = hfb_r.next()
                        ACTcopy(hfb[:], Hf[:], [r_Hf], [r_hfb])
                        bf_, r_bf = allb.next()
                        MM(bf_[:, 0:256], CT[:, c * 128:(c + 1) * 128], hfb[:], True, True, [r_CT, r_hfb], [r_bf])
                        bb2, r_bb2 = allb.next()
                        MM(bb2[:, 0:256], CT[:, c * 128:(c + 1) * 128], hbst[:, c, :], True, True, [r_CT, r_hbst[c]], [r_bb2])
                        if c < NT - 1:
                            bs, r_bs = allb.next()
                            MM(bs[:, 0:256], btm[:], xdd[:], True, True, [r_btm, r_xdd], [r_bs])
                            TT("dve", v4(Hf[:]), v4(Hf[:]), bc4(Eall[:, c, 32 + g * 4:36 + g * 4]), ALU.mult, [r_Hf, r_dt], [r_Hf])
                            TT("dve", Hf[:], bs[:, 0:256], Hf[:], ALU.add, [r_bs, r_Hf], [r_Hf])
                        y, r_y, _ = y_r.next()
                        t1, r_t1, _ = t1_r.next()
                        TT("dve", y[:], yd[:, 0:256], xD[:], ALU.add, [r_yd, r_xD], [r_y])
                        TT("dve", v4(t1[:]), v4(bf_[:, 0:256]), bc4(Eall[:, c, g * 4:g * 4 + 4]), ALU.mult, [r_bf, r_dt], [r_t1])
                        TT("dve", y[:], y[:], t1[:], ALU.add, [r_y, r_t1], [r_y])
                        TT("dve", v4(t1[:]), v4(bb2[:, 0:256]), bc4(Eall[:, c, 24 + g * 4:28 + g * 4]), ALU.mult, [r_bb2, r_dt], [r_t1])
                        TT("dve", y[:], y[:], t1[:], ALU.add, [r_y, r_t1], [r_y])
                        if dbg == "yssd":
                            DMA("sp", dbg_d[c * 128:(c + 1) * 128, g * 256:(g + 1) * 256], y[:], f"dbg{c % 4}", [r_y], [])
                        bz, r_bz = allb.next()
                        for kc in range(8):
                            MM(bz[:, 0:256], xT[:, kc, c * 128:(c + 1) * 128], wz[:, kc, :], kc == 0, kc == 7, [r_xT[c], r_wz], [r_bz])
                        sz, r_sz, _ = sz_r.next()
                        ACT(sz[:], bz[:, 0:256], AF.Silu, [r_bz], [r_sz])
                        yg, r_yg, _ = yg_r.next()
                        TT("dve", yg[:], y[:], sz[:], ALU.mult, [r_y, r_sz], [r_yg])
                        ssq, r_ssq, _ = ssq_r.next()
                        junk, r_junk, _ = junk_r.next()
                        MEMSET("pool", ssq[:], 0.0, [r_ssq])
                        ACT(junk[:], yg[:], AF.Square, [r_yg, r_ssq], [r_junk, r_ssq], accum_out=ssq[:, 0:1])
                        ACT(ssq[:, 1:2], ssq[:, 0:1], AF.Ln, [r_ssq], [r_ssq], bias=RMS_EPS, scale=1.0 / 256)
                        ACT(ssq[:, 2:3], ssq[:, 1:2], AF.Exp, [r_ssq], [r_ssq], scale=-0.5)
                        yn, r_yn, _ = yn_r.next()
                        STT("dve", yn[:], yg[:], ssq[:, 2:3], snw[:, g * 256:(g + 1) * 256], ALU.mult, ALU.mult, [r_yg, r_ssq, r_const], [r_yn])
                        if dbg == "ssd":
                            dtmp, r_dtmp, _ = junk_r.next()
                            CP("dve", dtmp[:], yn[:], [r_yn], [r_dtmp])
                            DMA("sp", dbg_d[c * 128:(c + 1) * 128, g * 256:(g + 1) * 256], dtmp[:], f"dbg{c % 4}", [r_dtmp], [])
                        bt_, r_bt = allb.next()
                        btv = bf16view(bt_, 8)
                        for j in range(2):
                            TR(btv[:, j, :], yn[:, j * 128:(j + 1) * 128], identb[:], [r_yn], [r_bt])
                        ynT, r_ynT, _ = ynT_r.next()
                        ACTcopy(ynT[:], btv[:, 0:2, :], [r_bt], [r_ynT])
                        for hf in range(2):
                            bo, r_bo = allb.next()
                            for j in range(2):
                                MM(bo[:], ynT[:, j, :], wos[:, j, hf * 512:(hf + 1) * 512], j == 0, j == 1, [r_ynT, r_wos], [r_bo])
                            TT("dve", acc[:, c, hf * 512:(hf + 1) * 512], bo[:], acc[:, c, hf * 512:(hf + 1) * 512], ALU.add,
                               [r_bo, r_acc[c]], [r_acc[c]])
                S.barrier()

        def ln_phase(idx):
            with ExitStack() as st:
                lnrot = Rot(st, "lnst", [128, 16], F32, 4)
                g_bc = sbt(st, "g_bc", [128, D], F32)
                b_bc = sbt(st, "b_bc", [128, D], F32)
                r_ln = S.res("lnp")
                load_ln(idx, g_bc, b_bc, r_ln)
                for t in range(NT):
                    layernorm(t, lnrot, g_bc, b_bc, r_ln)
                S.barrier()

        def ple_phase(s):
            with ExitStack() as st:
                xbrot = Rot(st, "xb", [128, D], BF16, 2)
                wgate = sbt(st, "wgate", [128, 8, D], BF16)
                wple = sbt(st, "wple", [128, 2, D], BF16)
                gb_bc = sbt(st, "gb_bc", [128, D], F32)
                r_w = S.res("plew")
                lnrot = Rot(st, "lnst", [128, 16], F32, 4)
                g_bc = sbt(st, "g_bc", [128, D], F32)
                b_bc = sbt(st, "b_bc", [128, D], F32)
                r_ln = S.res("lnp")
                pin_r = Rot(st, "pin", [128, 256], F32, 2)
                pb_r = Rot(st, "pb", [128, 256], BF16, 2)
                pT_r = Rot(st, "pT", [128, 2, 128], BF16, 2)
                gt_r = Rot(st, "gt", [128, 512], F32, 4)
                load_ln(3, g_bc, b_bc, r_ln)
                for h2 in range(2):
                    DMA("pool", wgate[:, :, h2 * 512:(h2 + 1) * 512], ple_gw[:, h2 * 512:(h2 + 1) * 512].rearrange("(kc p) f -> p kc f", p=128),
                        f"wgate{h2}", [], [r_w])
                DMA("pool", wple[:], ple_w.rearrange("(kc p) f -> p kc f", p=128), "wple", [], [r_w])
                DMA("sp", gb_bc[:], ple_gb.partition_broadcast(128), "gbb", [], [r_w])
                prep_banks = BankRot([6, 7])
                pt_banks = prep_banks
                g_banks = BankRot([0, 1, 2, 3])
                e_banks = BankRot([4, 5])
                tasks = []
                for t in range(NT):
                    S.begin_task()
                    prep(t, xbrot, prep_banks)
                    pin, r_pin, k_pin = pin_r.next()
                    DMA("sp", pin[:], p_d[s, t * 128:(t + 1) * 128, :], k_pin, [], [r_pin])
                    pb, r_pb, _ = pb_r.next()
                    CP("dve", pb[:], pin[:], [r_pin], [r_pb])
                    ptb, r_ptb = pt_banks.next()
                    ptv = bf16view(ptb, 8)
                    for kc in range(2):
                        TR(ptv[:, kc, :], pb[:, kc * 128:(kc + 1) * 128], identb[:], [r_pb], [r_ptb])
                    pT, r_pT, _ = pT_r.next()
                    ACTcopy(pT[:], ptv[:, 0:2, :], [r_ptb], [r_pT])
                    gbs = []
                    for hf in range(2):
                        gbk, r_gbk = g_banks.next()
                        ebk, r_ebk = e_banks.next()
                        gbs.append((gbk, r_gbk, ebk, r_ebk))
                        for kc in range(8):
                            MM(gbk[:], xT[:, kc, t * 128:(t + 1) * 128], wgate[:, kc, hf * 512:(hf + 1) * 512], kc == 0, kc == 7, [r_xT[t], r_w], [r_gbk])
                        for kc in range(2):
                            MM(ebk[:], pT[:, kc, :], wple[:, kc, hf * 512:(hf + 1) * 512], kc == 0, kc == 1, [r_pT, r_w], [r_ebk])
                    S.next_stage()
                    for hf in range(2):
                        gbk, r_gbk, ebk, r_ebk = gbs[hf]
                        gt, r_gt, _ = gt_r.next()
                        TT("dve", gt[:], gbk[:], gb_bc[:, hf * 512:(hf + 1) * 512], ALU.add, [r_gbk, r_w], [r_gt])
                        ACT(gt[:], gt[:], AF.Sigmoid, [r_gt], [r_gt])
                        TT("dve", gt[:], ebk[:], gt[:], ALU.mult, [r_ebk, r_gt], [r_gt])
                        TT("pool", acc[:, t, hf * 512:(hf + 1) * 512], acc[:, t, hf * 512:(hf + 1) * 512], gt[:], ALU.add, [r_acc[t], r_gt], [r_acc[t]])
                    S.next_stage()
                    layernorm(t, lnrot, g_bc, b_bc, r_ln)
                    DMA("sp", out_d[s, t * 128:(t + 1) * 128, :], acc[:, t, :], f"out{t}", [r_acc[t]], [])
                    tasks.append(S.end_task())
                S.replay_skewed(tasks)
                S.barrier()

        def dump_acc():
            for t in range(NT):
                DMA("sp", dbg_d[t * 128:(t + 1) * 128, :], acc[:, t, :], f"dbg{t % 4}", [r_acc[t]], [])

        def write_out(s):
            for t in range(NT):
                DMA("sp", out_d[s, t * 128:(t + 1) * 128, :], acc[:, t, :], f"out{t}", [r_acc[t]], [])
            S.barrier()

        for s in range(nseq):
            if stage == 30:
                with ExitStack() as st0:
                    xbrot0 = Rot(st0, "xb", [128, D], BF16, 2)
                    pb0 = BankRot([6, 7])
                    for t in range(NT):
                        DMA("sp", acc[:, t, :], x_d[s, t * 128:(t + 1) * 128, :], f"xin{t}", [], [r_acc[t]])
                        prep(t, xbrot0, pb0)
                    S.barrier()
                ssd_phase()
                if dbg == "acc":
                    dump_acc()
                write_out(s)
                continue
            ffn_phase(w1i, w1o, 0, load_x_seq=s)
            if stage == 1:
                if dbg == "acc" and s == 0:
                    dump_acc()
                write_out(s)
                continue
            attn_phase()
            if stage == 2:
                if dbg == "acc" and s == 0:
                    dump_acc()
                write_out(s)
                continue
            ssd_phase()
            if stage == 3:
                if dbg == "acc" and s == 0:
                    dump_acc()
                write_out(s)
                continue
            ln_phase(1)
            if stage == 4:
                if dbg == "acc" and s == 0:
                    dump_acc()
                write_out(s)
                continue
            ffn_phase(w2i, w2o, 2)
            if stage == 5:
                if dbg == "acc" and s == 0:
                    dump_acc()
                write_out(s)
                continue
            ple_phase(s)

        S.emit()
        print("sched stats", S.stats)
    return nc


def make_consts():
    i = np.arange(128)[:, None]
    j = np.arange(128)[None, :]
    masks = np.stack([(i <= j), (i >= j), (i > j), (i < j)]).astype(np.float32)
    rows = SEQ // 64
    row = np.repeat(np.arange(rows, dtype=np.float32), 64)
    col = np.tile(np.arange(64, dtype=np.float32), rows)
    inv = (10000.0 ** (-np.arange(0, 32, 2, dtype=np.float32) / 32)).astype(np.float32)
    ang = np.concatenate([row[:, None] * inv, col[:, None] * inv], axis=-1).astype(np.float32)
    return {"c_ident": np.eye(128, dtype=np.float32), "c_masks": masks,
            "c_cos": np.cos(ang).astype(np.float32), "c_sin": np.sin(ang).astype(np.float32)}


def make_in_maps(inputs, ncores, nseq):
    f = lambda a: np.ascontiguousarray(np.asarray(a, dtype=np.float32))
    shared = {
        "ln_g": f(inputs["ln_g"][0]), "ln_b": f(inputs["ln_b"][0]),
        "ffn1_w_in": f(inputs["ffn1_w_in"][0]), "ffn1_w_out": f(inputs["ffn1_w_out"][0]),
        "w_in": f(inputs["w_in"][0]), "q_norm": f(inputs["q_norm"][0]), "k_norm": f(inputs["k_norm"][0]),
        "conv_w": f(inputs["conv_w"][0]), "conv_b": f(inputs["conv_b"][0]),
        "dt_bias": f(inputs["dt_bias"][0]).reshape(16), "a_log": f(inputs["a_log"][0]).reshape(16),
        "d_skip": f(inputs["d_skip"][0]), "ssd_norm": f(inputs["ssd_norm"][0]),
        "w_out": f(inputs["w_out"][0]), "ffn2_w_in": f(inputs["ffn2_w_in"][0]), "ffn2_w_out": f(inputs["ffn2_w_out"][0]),
        "ple_w": f(inputs["ple_w"][0]), "ple_gate_w": f(inputs["ple_gate_w"][0]), "ple_gate_b": f(inputs["ple_gate_b"][0]),
    }
    shared.update(make_consts())
    x = np.asarray(inputs["x"], dtype=np.float32)
    p = np.asarray(inputs["p"], dtype=np.float32)[0]
    maps = []
    for c in range(ncores):
        m = dict(shared)
        m["x"] = np.ascontiguousarray(x[c * nseq:(c + 1) * nseq])
        m["p"] = np.ascontiguousarray(p[c * nseq:(c + 1) * nseq])
        maps.append(m)
    return maps


def kernel(**inputs):
    ncores, nseq = 8, 2
    nc = build_nc(nseq=nseq)
    maps = make_in_maps(inputs, ncores, nseq)
    res = run_bass_kernel_spmd(nc, maps, core_ids=list(range(ncores)))
    out = np.concatenate([np.asarray(r["out"]) for r in res.results], axis=0)
    return out.astype(np.float32)
```
